# Optimizing a Trainium2 kernel written in Bass

```python
import jax, jax.numpy as jnp
from jax import lax
import numpy as np

D_MODEL = 2048
BATCH = 1
SEQ = 16384
DEPTH = 2
DEC_BATCH = 32
DEC_SEQ = 32
PAST_LEN = 4096

CHUNK = 64
LEFT_CHUNKS = 8
BAND_CHUNKS = LEFT_CHUNKS + 1
BAND_KEYS = BAND_CHUNKS * CHUNK
ATTN_KEEP = LEFT_CHUNKS * CHUNK
QB_CHUNKS = 2
N_HEADS = 32
HEAD_DIM = 64
D_ATTN = N_HEADS * HEAD_DIM
REL_CLIP = 128
N_REL = 2 * REL_CLIP + 1
D_CONV = D_MODEL
CONV_WIDTH = 31
CONV_KEEP = CONV_WIDTH - 1
N_MIXERS = 2
N_ATTN_LAYERS = (DEPTH + 1) // 2
N_CONV_LAYERS = DEPTH // 2
RMS_EPS = 1e-6
LN_EPS = 1e-5
NEG_INF = -1e30

kernel_name = "chunk_stream_attn_conv_hybrid_step"


def _rmsnorm(x, g):
    xf = x.astype(jnp.float32)
    y = xf * lax.rsqrt(jnp.mean(xf * xf, axis=-1, keepdims=True) + RMS_EPS)
    return (y * g.astype(jnp.float32)).astype(x.dtype)


def _layernorm(x, g, b):
    xf = x.astype(jnp.float32)
    mu = jnp.mean(xf, axis=-1, keepdims=True)
    xc = xf - mu
    var = jnp.mean(xc * xc, axis=-1, keepdims=True)
    return (xc * lax.rsqrt(var + LN_EPS) * g.astype(jnp.float32) + b.astype(jnp.float32)).astype(x.dtype)


def _rel_bias(rel_table, q_pos, k_pos):
    rel = jnp.clip(q_pos[:, None] - k_pos[None, :], -REL_CLIP, REL_CLIP) + REL_CLIP
    return jnp.transpose(rel_table[rel], (2, 0, 1)).astype(jnp.float32)


def _chunk_band_valid(q_abs, k_abs):
    qc = (q_abs // CHUNK)[..., :, None]
    kc = (k_abs // CHUNK)[..., None, :]
    return (k_abs[..., None, :] >= 0) & (kc <= qc) & (kc >= qc - LEFT_CHUNKS)


def _band_attend(q, kb, vb, bias, valid):
    s = jnp.einsum('bnqhd,bnkhd->bnhqk', q, kb).astype(jnp.float32) * (HEAD_DIM ** -0.5) + bias
    s = jnp.where(valid[None, :, None], s, NEG_INF)
    p = jax.nn.softmax(s, axis=-1).astype(vb.dtype)
    return jnp.einsum('bnhqk,bnkhd->bnqhd', p, vb)


def _qkvg(hn, w_in):
    q, k, v, g = jnp.split(hn @ w_in, 4, axis=-1)
    hs = hn.shape[:2] + (N_HEADS, HEAD_DIM)
    return q.reshape(hs), k.reshape(hs), v.reshape(hs), g


def _attn_prompt(hn, w_in, w_out, rel_table):
    B, S, _ = hn.shape
    nc = S // CHUNK
    nb = nc // QB_CHUNKS
    q, k, v, g = _qkvg(hn, w_in)
    shp = (B, nc, CHUNK, N_HEADS, HEAD_DIM)
    qc = q.reshape(shp)
    pad = ((0, 0), (LEFT_CHUNKS, 0), (0, 0), (0, 0), (0, 0))
    kp = jnp.pad(k.reshape(shp), pad)
    vp = jnp.pad(v.reshape(shp), pad)
    band_idx = np.arange(QB_CHUNKS)[:, None] + np.arange(BAND_CHUNKS)[None, :]
    loc_q = jnp.arange(CHUNK)
    loc_k = jnp.arange(BAND_KEYS) - LEFT_CHUNKS * CHUNK
    bias = _rel_bias(rel_table, loc_q, loc_k)

    def block(bi):
        c0 = bi * QB_CHUNKS
        qb = lax.dynamic_slice_in_dim(qc, c0, QB_CHUNKS, axis=1)
        kw = lax.dynamic_slice_in_dim(kp, c0, QB_CHUNKS + LEFT_CHUNKS, axis=1)
        vw = lax.dynamic_slice_in_dim(vp, c0, QB_CHUNKS + LEFT_CHUNKS, axis=1)
        kb = kw[:, band_idx].reshape(B, QB_CHUNKS, BAND_KEYS, N_HEADS, HEAD_DIM)
        vb = vw[:, band_idx].reshape(B, QB_CHUNKS, BAND_KEYS, N_HEADS, HEAD_DIM)
        chunk_ids = c0 + jnp.arange(QB_CHUNKS)
        q_abs = chunk_ids[:, None] * CHUNK + loc_q[None, :]
        k_abs = chunk_ids[:, None] * CHUNK + loc_k[None, :]
        return _band_attend(qb, kb, vb, bias, _chunk_band_valid(q_abs, k_abs))

    o = lax.map(block, jnp.arange(nb))
    o = jnp.moveaxis(o, 0, 1).reshape(B, S, D_ATTN)
    y = (o * jax.nn.silu(g)) @ w_out
    keep = min(ATTN_KEEP, S)
    return y, k[:, S - keep:], v[:, S - keep:]


def _attn_sample(hn, cache_k, cache_v, w_in, w_out, rel_table):
    Bd, T, _ = hn.shape
    L = cache_k.shape[1]
    q, k, v, g = _qkvg(hn, w_in)
    kc = jnp.concatenate([cache_k, k], axis=1)
    vc = jnp.concatenate([cache_v, v], axis=1)
    q_abs = PAST_LEN + jnp.arange(T)
    k_abs = PAST_LEN - L + jnp.arange(L + T)
    bias = _rel_bias(rel_table, q_abs, k_abs)
    valid = _chunk_band_valid(q_abs, k_abs)[None]
    o = _band_attend(q[:, None], kc[:, None], vc[:, None], bias, valid)[:, 0]
    y = (o.reshape(Bd, T, D_ATTN) * jax.nn.silu(g)) @ w_out
    return y, kc[:, T:], vc[:, T:]


def _conv_module(hn, left, w_in, b_in, w_dw, b_dw, ln_g, ln_b, w_out):
    a, b, g = jnp.split(hn @ w_in + b_in, 3, axis=-1)
    u = a * jax.nn.sigmoid(b)
    up = jnp.concatenate([left, u], axis=1)
    c = lax.conv_general_dilated(up, w_dw[:, None, :], window_strides=(1,), padding='VALID',
                                 dimension_numbers=('NWC', 'WIO', 'NWC'),
                                 feature_group_count=D_CONV) + b_dw
    z = jax.nn.silu(_layernorm(c, ln_g, ln_b)) * jax.nn.silu(g)
    return z @ w_out, up[:, -CONV_KEEP:]


def _trunk(x, is_sample, cache_k, cache_v, cache_conv, norm_g, final_g, attn_w_in, attn_w_out,
           attn_rel_bias, conv_w_in, conv_b_in, conv_w_dw, conv_b_dw, conv_ln_g, conv_ln_b, conv_w_out):
    h = x
    new_k, new_v, new_c = [], [], []
    for i in range(DEPTH):
        hn = _rmsnorm(h, norm_g[i])
        j = i // N_MIXERS
        if i % N_MIXERS == 0:
            if is_sample:
                y, nk, nv = _attn_sample(hn, cache_k[j], cache_v[j], attn_w_in[j], attn_w_out[j], attn_rel_bias[j])
            else:
                y, nk, nv = _attn_prompt(hn, attn_w_in[j], attn_w_out[j], attn_rel_bias[j])
            new_k.append(nk)
            new_v.append(nv)
        else:
            left = cache_conv[j] if is_sample else jnp.zeros((x.shape[0], CONV_KEEP, D_CONV), hn.dtype)
            y, nc = _conv_module(hn, left, conv_w_in[j], conv_b_in[j], conv_w_dw[j], conv_b_dw[j],
                                 conv_ln_g[j], conv_ln_b[j], conv_w_out[j])
            new_c.append(nc)
        h = h + y
    return _rmsnorm(h, final_g), jnp.stack(new_k), jnp.stack(new_v), jnp.stack(new_c)


def setup_inputs(seed: int = 0) -> dict:
    key = jax.random.key(seed)
    ks = jax.random.split(key, 18)

    def nrm(k, shape, s):
        return jax.random.normal(k, shape, jnp.float32) * s

    keep = min(ATTN_KEEP, PAST_LEN)
    return {
        'x_prompt': nrm(ks[0], (BATCH, SEQ, D_MODEL), 1.0),
        'x_sample': nrm(ks[1], (DEC_BATCH, DEC_SEQ, D_MODEL), 1.0),
        'cache_attn_k': nrm(ks[2], (N_ATTN_LAYERS, DEC_BATCH, keep, N_HEADS, HEAD_DIM), 1.0),
        'cache_attn_v': nrm(ks[3], (N_ATTN_LAYERS, DEC_BATCH, keep, N_HEADS, HEAD_DIM), 1.0),
        'cache_conv': nrm(ks[4], (N_CONV_LAYERS, DEC_BATCH, CONV_KEEP, D_CONV), 0.5),
        'norm_g': 1.0 + nrm(ks[5], (DEPTH, D_MODEL), 0.02),
        'final_g': 1.0 + nrm(ks[6], (D_MODEL,), 0.02),
        'attn_w_in': nrm(ks[7], (N_ATTN_LAYERS, D_MODEL, 4 * D_ATTN), D_MODEL ** -0.5),
        'attn_w_out': nrm(ks[8], (N_ATTN_LAYERS, D_ATTN, D_MODEL), D_ATTN ** -0.5),
        'attn_rel_bias': nrm(ks[9], (N_ATTN_LAYERS, N_REL, N_HEADS), 0.1),
        'conv_w_in': nrm(ks[10], (N_CONV_LAYERS, D_MODEL, 3 * D_CONV), D_MODEL ** -0.5),
        'conv_b_in': nrm(ks[11], (N_CONV_LAYERS, 3 * D_CONV), 0.01),
        'conv_w_dw': nrm(ks[12], (N_CONV_LAYERS, CONV_WIDTH, D_CONV), CONV_WIDTH ** -0.5),
        'conv_b_dw': nrm(ks[13], (N_CONV_LAYERS, D_CONV), 0.01),
        'conv_ln_g': 1.0 + nrm(ks[14], (N_CONV_LAYERS, D_CONV), 0.02),
        'conv_ln_b': nrm(ks[15], (N_CONV_LAYERS, D_CONV), 0.01),
        'conv_w_out': nrm(ks[16], (N_CONV_LAYERS, D_CONV, D_MODEL), D_CONV ** -0.5),
    }


def reference(x_prompt, x_sample, cache_attn_k, cache_attn_v, cache_conv, norm_g, final_g,
              attn_w_in, attn_w_out, attn_rel_bias, conv_w_in, conv_b_in, conv_w_dw, conv_b_dw,
              conv_ln_g, conv_ln_b, conv_w_out):
    y_prompt, k_p, v_p, c_p = _trunk(x_prompt, False, None, None, None, norm_g, final_g,
                                     attn_w_in, attn_w_out, attn_rel_bias, conv_w_in, conv_b_in,
                                     conv_w_dw, conv_b_dw, conv_ln_g, conv_ln_b, conv_w_out)
    y_sample, k_s, v_s, c_s = _trunk(x_sample, True, cache_attn_k, cache_attn_v, cache_conv, norm_g,
                                     final_g, attn_w_in, attn_w_out, attn_rel_bias, conv_w_in,
                                     conv_b_in, conv_w_dw, conv_b_dw, conv_ln_g, conv_ln_b, conv_w_out)
    return (y_prompt, y_sample, k_p, v_p, c_p, k_s, v_s, c_s)
```

```python
import contextlib
import numpy as np
import concourse.bass as bass
import concourse.mybir as mybir
from concourse.bass_utils import run_bass_kernel_spmd

F32 = mybir.dt.float32
BF16 = mybir.dt.bfloat16
AF = mybir.ActivationFunctionType
ALU = mybir.AluOpType

ENGS = ("pe", "act", "dve", "pool", "sp")
SEM_CAP = 12000
N_DMA_SEMS = 12

NCORES = 8
D = 2048
NH = 32
HD = 64
NHP = 16
OWN = 2048
HALO_BLKS = 5
NPB = HALO_BLKS + OWN // 128
NB = 3
NSLOT = NB + 4
TMAX = NB * 128
CW = 31
CK = 30
NPRM = 39
RMS_EPS = 1e-6
LN_EPS = 1e-5


class Op:
    __slots__ = ("eng", "fn", "dma", "deps", "flag", "tok", "idx", "pre_wait")

    def __init__(self, eng, fn, dma):
        self.eng = eng
        self.fn = fn
        self.dma = dma
        self.deps = []
        self.flag = False
        self.tok = None
        self.pre_wait = None


class Prog:
    def __init__(self, nc):
        self.nc = nc
        self.ops = []
        self.lastw = {}
        self.readers = {}
        self.stack = contextlib.ExitStack()

    def sb(self, name, shape, dt):
        return self.stack.enter_context(self.nc.sbuf_tensor("s_" + name, list(shape), dt))

    def ps(self, name, shape, dt=F32):
        return self.stack.enter_context(self.nc.psum_tensor("p_" + name, list(shape), dt))

    def op(self, eng, fn, reads=(), writes=(), dma=False):
        o = Op(eng, fn, dma)
        o.idx = len(self.ops)
        psk = [r for r in reads if isinstance(r, tuple) and str(r[0]).startswith("ps")]
        if psk:
            reads = [r for r in reads if r not in psk]
            writes = list(writes) + psk
        deps = {}
        for r in reads:
            w = self.lastw.get(r)
            if w is not None:
                deps[w.idx] = (w, True)
        for r in writes:
            w = self.lastw.get(r)
            if w is not None and w.idx not in deps:
                deps[w.idx] = (w, False)
            for rd in self.readers.get(r, ()):
                if rd.idx not in deps:
                    deps[rd.idx] = (rd, False)
        for p, raw in deps.values():
            if p is o:
                continue
            if p.eng == o.eng and not p.dma and not o.dma:
                if o.eng == "pe":
                    continue
            p.flag = True
            o.deps.append(p)
        for r in reads:
            self.readers.setdefault(r, []).append(o)
        for r in writes:
            self.lastw[r] = o
            self.readers[r] = []
        self.ops.append(o)
        return o

    def pe(self, fn, reads=(), writes=()):
        return self.op("pe", fn, reads, writes)

    def act(self, fn, reads=(), writes=()):
        return self.op("act", fn, reads, writes)

    def dve(self, fn, reads=(), writes=()):
        return self.op("dve", fn, reads, writes)

    def pool(self, fn, reads=(), writes=()):
        return self.op("pool", fn, reads, writes)

    def dma(self, q, out, in_, reads=(), writes=()):
        return self.op(q, lambda e: e.dma_start(out=out, in_=in_), reads, writes, dma=True)

    def emit(self):
        nc = self.nc
        cnt = {e: 0 for e in ENGS}
        dcnt = {e: 0 for e in ENGS}
        n_csem = {e: 0 for e in ENGS}
        for o in self.ops:
            if o.dma:
                k = dcnt[o.eng]
                dcnt[o.eng] += 1
                slot = k % N_DMA_SEMS
                use = k // N_DMA_SEMS
                o.tok = (("d", o.eng, slot), 16 * (use + 1))
                if use > 0:
                    o.pre_wait = (("d", o.eng, slot), 16 * use)
                o.flag = True
            elif o.flag:
                n = cnt[o.eng]
                cnt[o.eng] += 1
                o.tok = (("c", o.eng, n // SEM_CAP), n % SEM_CAP + 1)
                n_csem[o.eng] = n // SEM_CAP + 1
        sems = {}
        for e in ENGS:
            for i in range(n_csem[e]):
                sems[("c", e, i)] = self.stack.enter_context(nc.semaphore(f"c_{e}_{i}"))
            for s in range(min(N_DMA_SEMS, dcnt[e])):
                sems[("d", e, s)] = self.stack.enter_context(nc.semaphore(f"d_{e}_{s}"))
        final = {}
        for o in self.ops:
            if o.dma:
                final[o.tok[0]] = max(final.get(o.tok[0], 0), o.tok[1])
        block = self.stack.enter_context(nc.Block())
        ops = self.ops
        stats = {}

        def run(engname, eng):
            waited = {}
            nw = 0
            mine = [o for o in ops if o.eng == engname]
            for o in mine:
                reqs = []
                if o.pre_wait is not None:
                    reqs.append(o.pre_wait)
                for p in o.deps:
                    reqs.append(p.tok)
                need = {}
                for (sk, v) in reqs:
                    if v > need.get(sk, 0):
                        need[sk] = v
                for sk, v in need.items():
                    if waited.get(sk, 0) >= v:
                        continue
                    waited[sk] = v
                    eng.wait_ge(sems[sk], v)
                    nw += 1
                ins = o.fn(eng)
                if o.flag:
                    ins.then_inc(sems[o.tok[0]], 16 if o.dma else 1)
            for sk, v in final.items():
                if sk[1] == engname and waited.get(sk, 0) < v:
                    eng.wait_ge(sems[sk], v)
            stats[engname] = (len(mine), nw)

        block.sync(lambda e: run("sp", e))
        block.scalar(lambda e: run("act", e))
        block.vector(lambda e: run("dve", e))
        block.gpsimd(lambda e: run("pool", e))
        block.tensor(lambda e: run("pe", e))
        self.stats = stats
        self.stack.close()
        return nc


def build_program(max_full_tiles=None):
    nc = bass.Bass("TRN2", target_bir_lowering=False)
    P = Prog(nc)

    def din(name, shape, dt=F32):
        return nc.dram_tensor(name, list(shape), dt, kind="ExternalInput").ap()

    def dout(name, shape, dt=F32):
        return nc.dram_tensor(name, list(shape), dt, kind="ExternalOutput").ap()

    def dscr(name, shape, dt=BF16):
        return nc.dram_tensor(name, list(shape), dt, kind="Internal").ap()

    xp = din("xp", [NPB * 128, D])
    xs = din("xs", [128, D])
    ck = din("ck", [4, 512, D])
    cv = din("cv", [4, 512, D])
    ccv = din("ccv", [4 * CK, D])
    w_ai = din("w_ai", [D, 4 * D])
    w_ao = din("w_ao", [D, D])
    w_ci = din("w_ci", [D, 3 * D])
    w_co = din("w_co", [D, D])
    prm = din("prm", [NPRM, D])
    fgd = din("fg", [128, D])
    relT = din("relT", [128, 2, NH, 128])
    cvd = din("cvec", [128, NH])
    idd = din("ident", [128, 128])
    hmd = din("hmask", [128, 1])

    yp = dout("yp", [OWN, D])
    ysd = dout("ys", [128, D])
    kvp = dout("kvp", [2, 512, D])
    kvs = dout("kvs", [2, 128, D])
    cpd = dout("cp", [CK, D])
    csd = dout("cs", [4, CK, D])
    kso = dout("kso", [4, 480, D])
    vso = dout("vso", [4, 480, D])

    scrA = dscr("scrA", [NHP, 128, 4, 16, 128])
    scrO = dscr("scrO", [4, 128, 16, 512])
    scrB = dscr("scrB", [16, 128, 3, 16, 128])
    scrP = dscr("scrP", [4, 128, 16, 512])
    scrE = dscr("scrE", [NHP, 128, 2, 2, 128])

    identf = P.sb("identf", [128, 128], F32)
    identb = P.sb("identb", [128, 128], BF16)
    prmT = P.sb("prmT", [128, 16, NPRM], F32)
    prmH = P.sb("prmH", [128, 16, 5], F32)
    cvec = P.sb("cvec", [128, NH], F32)
    hmask = P.sb("hmask", [128, 1], F32)
    twos = P.sb("twos", [128, NH, 1], BF16)
    fdum = P.sb("fdum", [128, 8], F32)
    Eb = [P.sb(f"Eb{j}", [128, 2, 2, 128], BF16) for j in range(2)]
    h = P.sb("h", [128, NB, D], F32)
    xb = [P.sb(f"xb{j}", [128, D], BF16) for j in range(2)]
    ssq = P.sb("ssq", [128, NB], F32)
    rstd = P.sb("rstd", [128, NB], F32)
    hnT = P.sb("hnT", [128, 16, TMAX], BF16)
    KT = P.sb("KT", [128, NHP, NSLOT, 128], BF16)
    V = P.sb("V", [128, NSLOT, NH, HD + 1], BF16)
    L0N = 6 * TMAX + 2048 + 256
    l0t = P.sb("l0t", [128, max(L0N, CW * 128)], BF16)
    _o = {"o": 0}

    def carve(n):
        v = l0t[:, _o["o"]:_o["o"] + n]
        _o["o"] += n
        return v
    QT = [carve(TMAX) for j in range(2)]
    VT = [carve(TMAX) for j in range(2)]
    sgT = [carve(TMAX) for j in range(2)]
    et = [carve(256).rearrange("p (a b) -> p a b", b=128) for j in range(2)]
    Pb = [carve(640).rearrange("p (a b) -> p a b", b=128) for j in range(2)]
    on = [carve(128).rearrange("p (a b) -> p a b", b=HD) for j in range(2)]
    KTn = [carve(128) for j in range(2)]
    dgb = [l0t[:, 0:CW * 128].rearrange("p (a b) -> p a b", b=128)]
    L0KEYS = [(nm, j) for nm in ("QT", "KTn", "VT", "sgT", "et", "P", "on") for j in range(2)]
    rec = [P.sb(f"rec{j}", [128, 2, 1], F32) for j in range(2)]
    ogcz = P.sb("ogcz", [128, 16, TMAX], BF16)
    NW = 2
    wbuf = [P.sb(f"wbuf{j}", [128, 16, 512], BF16) for j in range(NW)]
    kvo = [P.sb("kvo0", [128, 2, 128], F32)] * 2
    UW = max(CK + TMAX, CK + (NB - 1) * 128 + 4 * (CK + 32))
    ub = [P.sb("ub0", [128, UW], F32)] * 2
    ust = P.sb("ust", [128, 16, CK], F32)
    ah = [P.sb(f"ah{j}", [128, TMAX], F32) for j in range(2)]
    tg = ah
    tb = [P.sb(f"tb{j}", [128, TMAX], F32) for j in range(2)]
    ubb = [P.sb(f"ubb{j}", [128, UW], BF16) for j in range(2)]
    c2 = [P.sb(f"c2_{j}", [128, TMAX], BF16) for j in range(2)]
    sg1 = P.sb("sg1", [128, 16, TMAX], BF16)
    mu = P.sb("mu", [128, TMAX], F32)
    rsd = P.sb("rsd", [128, TMAX], F32)
    t1 = ah
    tt = [P.sb(f"tt_{j}", [128, TMAX], F32) for j in range(2)]
    vh = [P.sb(f"vh_{j}", [128, TMAX], F32) for j in range(2)]
    cst = P.sb("cst", [32, 4, 128], F32)
    kcf = [P.sb("kcf0", [128, 4, 128], F32)] * 2
    kcb = [P.sb(f"kcb{j}", [128, 4, 128], BF16) for j in range(2)]
    KTc = [P.sb(f"KTc{j}", [128, 512], BF16) for j in range(2)]
    vcf = [P.sb("vcf0", [128, 4, 128], F32)] * 2
    Vc = [P.sb(f"Vc{j}", [128, 4, 2, HD + 1], BF16) for j in range(2)]
    Vn = [P.sb(f"Vn{j}", [32, 2, HD + 1], BF16) for j in range(2)]

    psS = [P.ps(f"psS{j}", [128, 512], F32) for j in range(3)]
    psP = [P.ps(f"psP{j}", [128, 512], F32) for j in range(2)]
    psO = P.ps("psO", [128, 512], F32)
    psT = [P.ps(f"psT{j}", [128, 1024], BF16) for j in range(2)]

    HS = lambda j: [("hstg", j, a) for a in range(4)]
    ctr = {"w": 0, "p": 0, "t": 0, "x": 0, "kvo": 0, "u": 0, "s": 0}

    def nxt(k, n):
        v = ctr[k]
        ctr[k] = v + 1
        return v % n

    P.dma("sp", identf[:], idd, writes=["identf"])
    P.dma("sp", cvec[:], cvd, writes=["cvec"])
    P.dma("sp", hmask[:], hmd, writes=["hmask"])
    P.dve(lambda e: e.tensor_copy(out=identb[:], in_=identf[:]), reads=["identf"], writes=["identb"])
    P.pool(lambda e: e.memset(twos[:], 2.0), writes=["twos"])
    P.pool(lambda e: e.memset(fdum[:], 0.0), writes=["fdum"])
    for j in range(2):
        P.pool(lambda e, j=j: e.memset(Pb[j][:], 0.0), writes=[("P", j)])
    P.pool(lambda e: e.memset(ust[:], 0.0), writes=[("ust", cc) for cc in range(16)])
    for j in range(2):
        P.pool(lambda e, j=j: e.memset(Vc[j][:], 2.0), writes=[("Vc", j)])
        P.pool(lambda e, j=j: e.memset(Vn[j][:], 2.0), writes=[("Vn", j)])
    P.dma("sp", h[0:NPRM, 0, :], prm, writes=HS(0))
    for g in range(2):
        for c8 in range(8):
            cc = g * 8 + c8
            P.pe(lambda e, cc=cc, c8=c8: e.matmul(
                psS[0][:, c8 * NPRM:(c8 + 1) * NPRM], lhsT=h[0:NPRM, 0, cc * 128:(cc + 1) * 128],
                rhs=identf[0:NPRM, 0:NPRM], start=True, stop=True),
                reads=HS(0) + [("h", 0), "identf"], writes=[("psS", 0)])
        P.dve(lambda e, g=g: e.tensor_copy(
            out=prmT[:, g * 8:(g + 1) * 8, :],
            in_=psS[0][:, 0:8 * NPRM].rearrange("p (c r) -> p c r", r=NPRM)),
            reads=[("psS", 0)], writes=["prmT"])
    P.act(lambda e: e.mul(out=prmH[:], in_=prmT[:, :, 32:37], mul=0.5), reads=["prmT"], writes=["prmH"])
    for dl in range(2):
        for hg in range(4):
            j = (dl * 4 + hg) % 2
            stg = h[:, j, 0:1024].rearrange("p (a b) -> p a b", b=128)
            xv = xb[j][:, 0:1024].rearrange("p (a b) -> p a b", b=128)
            P.dma("sp", stg, relT[:, dl, hg * 8:(hg + 1) * 8, :], writes=HS(j))
            P.act(lambda e, stg=stg, xv=xv: e.activation(out=xv, in_=stg, func=AF.Exp),
                  reads=HS(j) + [("h", j)], writes=[("xb", j)])
            if dl == 0:
                P.pool(lambda e, xv=xv: e.memset(xv[64:128, :, 0:64], 0.0), reads=[("xb", j)], writes=[("xb", j)])
            P.dma("pool", scrE[hg * 4:(hg + 1) * 4, :, dl, :, :].rearrange("hp p e q -> p hp e q"),
                  xb[j][:, 0:1024].rearrange("p (hp e q) -> p hp e q", hp=4, e=2), reads=[("xb", j)],
                  writes=[("scrE", hg * 4 + i4) for i4 in range(4)])

    cast_rr = {"i": 0}
    KTf = KT[:].rearrange("p a b c -> p (a b c)")
    stgF = [h[:, i, :] for i in range(NB)]
    stgF += [KTf[:, k * 4096:(k + 1) * 4096].bitcast(F32) for k in range(3)]
    for t_ in (hnT, ogcz, sg1):
        stgF.append(t_[:].rearrange("p a b -> p (a b)")[:, 0:4096].bitcast(F32))
    stgB = [xb[0][:], xb[1][:]]
    for t_ in (wbuf[0], wbuf[1]):
        tf = t_[:].rearrange("p a b -> p (a b)")
        stgB += [tf[:, k * 2048:(k + 1) * 2048] for k in range(4)]
    NSTG = min(len(stgF), len(stgB))
    STGKEYS = [("stgF", k, a) for k in range(NSTG) for a in range(4)] + [("stgB", k) for k in range(NSTG)]

    def convert_piece(src, out_view_of_B, n, store_dst, store_src_of_B, wkeys):
        k = nxt("x", NSTG)
        P.dma("sp", stgF[k][:, 0:n].rearrange("p (a w) -> p a w", a=4), src, writes=[("stgF", k, a) for a in range(4)])
        ce = cast_rr["i"] % 3
        cast_rr["i"] += 1
        eng = ("act", "dve", "pool")[ce]
        ov = out_view_of_B(stgB[k][:, 0:n])
        iv = stgF[k][:, 0:n] if len(ov.shape) == 2 else stgF[k][:, 0:n].rearrange("p (kc hp m) -> p kc hp m", kc=4, m=128)
        if ce == 0:
            fn = lambda e: e.copy(out=ov, in_=iv)
        else:
            fn = lambda e: e.tensor_copy(out=ov, in_=iv)
        P.op(eng, fn, reads=[("stgF", k, a) for a in range(4)], writes=[("stgB", k)])
        P.dma("pool", store_dst, store_src_of_B(stgB[k][:, 0:n]), reads=[("stgB", k)], writes=wkeys)

    def convert_sectioned(W, nsec, scr, kname):
        Wv = W.rearrange("(kc p) n -> p kc n", p=128)
        for j in range(nsec):
            for g in range(4):
                for kg in range(4):
                    src = Wv[:, kg * 4:(kg + 1) * 4, j * D + g * 512:j * D + (g + 1) * 512]
                    dst = scr[g * 4:(g + 1) * 4, :, j, kg * 4:(kg + 1) * 4, :].rearrange("hp p kc m -> p hp kc m")
                    convert_piece(src, lambda b: b.rearrange("p (hp kc m) -> p kc hp m", hp=4, m=128), 2048, dst,
                                  lambda b: b.rearrange("p (hp kc m) -> p hp kc m", hp=4, m=128),
                                  [(kname, g * 4 + i4) for i4 in range(4)])

    def convert_plain(W, scr, kname):
        Wv = W.rearrange("(kc p) n -> p kc n", p=128)
        for s_ in range(4):
            for kg in range(4):
                convert_piece(Wv[:, kg * 4:(kg + 1) * 4, s_ * 512:(s_ + 1) * 512], lambda b: b, 2048,
                              scr[s_, :, kg * 4:(kg + 1) * 4, :], lambda b: b.rearrange("p (a w) -> p a w", a=4),
                              [(kname, s_)])

    Wai = w_ai.rearrange("(kc p) n -> p kc n", p=128)
    Wci = w_ci.rearrange("(kc p) n -> p kc n", p=128)
    Wao = w_ao.rearrange("(kc p) n -> p kc n", p=128)
    Wco = w_co.rearrange("(kc p) n -> p kc n", p=128)
    for hp in range(NHP):
        for j in (1, 2):
            P.dma("pool", scrA[hp, :, j, :, :], Wai[:, :, j * D + hp * 128:j * D + (hp + 1) * 128],
                  writes=[("scrA", hp, j)])
    cvq = []
    for hp in range(NHP):
        for j in (0, 3):
            cvq.append((scrA[hp, :, j, :, :], Wai[:, :, j * D + hp * 128:j * D + (hp + 1) * 128], ("scrA", hp, j)))
    for s_ in range(4):
        for kg in range(4):
            cvq.append((scrO[s_, :, kg * 4:(kg + 1) * 4, :], Wao[:, kg * 4:(kg + 1) * 4, s_ * 512:(s_ + 1) * 512],
                        ("scrO", s_, kg)))
    for cc in range(16):
        for j in range(3):
            cvq.append((scrB[cc, :, j, :, :], Wci[:, :, j * D + cc * 128:j * D + (cc + 1) * 128], ("scrB", cc, j)))
    for s_ in range(4):
        for kg in range(4):
            cvq.append((scrP[s_, :, kg * 4:(kg + 1) * 4, :], Wco[:, kg * 4:(kg + 1) * 4, s_ * 512:(s_ + 1) * 512],
                        ("scrP", s_, kg)))

    def cv_pump(n, gate):
        for _ in range(n):
            if cvq:
                dst, src, wkey = cvq.pop(0)
                P.dma("pool", dst, src, reads=[gate], writes=[wkey])

    KA = lambda s: ("scrA", s)
    KO = lambda s: ("scrO", s)
    KB = lambda s: ("scrB", s)
    KP = lambda s: ("scrP", s)

    for s in range(4):
        P.dma("pool", kso[s], ck[s, 32:512, :])
        P.dma("pool", vso[s], cv[s, 32:512, :])

    slab_plan = []
    slab_state = {"cur": 0, "issued": 0}
    SCR = {"A": (scrA, KA), "O": (scrO, KO), "B": (scrB, KB), "P": (scrP, KP)}

    def issue_slab(idx):
        nm, s_, c0, c1 = slab_plan[idx]
        scr, key = SCR[nm]
        j = idx % NW
        wf = wbuf[j][:].rearrange("p a b -> p (a b)")
        if nm == "A":
            j0, j1 = (0, 4) if c0 == 0 else (1, 3)
            P.dma("sp", wf[:, j0 * 2048:j1 * 2048], scr[s_, :, j0:j1, :, :].rearrange("p j kc m -> p (j kc m)"),
                  reads=[("scrA", s_, jj) for jj in range(j0, j1)], writes=[("wbuf", j)])
        elif nm == "B":
            P.dma("sp", wf[:, 0:3 * 2048], scr[s_].rearrange("p j kc m -> p (j kc m)"),
                  reads=[("scrB", s_, jj) for jj in range(3)], writes=[("wbuf", j)])
        else:
            P.dma("sp", wbuf[j][:, :, c0:c1], scr[s_, :, :, c0:c1], reads=[(key(s_)[0], s_, kg) for kg in range(4)],
                  writes=[("wbuf", j)])

    def load_slab(scr, key, s, width=512, c0=0, c1=None):
        c1 = width if c1 is None else c1
        idx = slab_state["cur"]
        slab_state["cur"] = idx + 1
        nm = [k for k, v in SCR.items() if v[0] is scr][0]
        assert slab_plan[idx] == (nm, s, c0, c1), (slab_plan[idx], (nm, s, c0, c1))
        if nm == "A" and c0 == 0:
            P.dma("sp", Eb[s % 2][:], scrE[s], reads=[("scrE", s)], writes=[("Eb", s % 2)])
        while slab_state["issued"] <= min(idx + 1, len(slab_plan) - 1):
            issue_slab(slab_state["issued"])
            slab_state["issued"] += 1
        return idx % NW

    def w_unit(wj, col0, T, evac):
        pj = nxt("p", 2)
        wsec = wbuf[wj][:].rearrange("p a b -> p (a b)").rearrange("p (j kc m) -> p j kc m", j=4, kc=16)
        for kc in range(16):
            P.pe(lambda e, kc=kc: e.matmul(psP[pj][:, 0:T], lhsT=wsec[:, col0 // 128, kc, :],
                                           rhs=hnT[:, kc, 0:T], start=(kc == 0), stop=(kc == 15)),
                 reads=[("wbuf", wj)] + [("hnT", i) for i in range(NB)], writes=[("psP", pj)])
        evac(psP[pj], ("psP", pj))

    def rmsnorm_T(nb, grow):
        for i in range(nb):
            j = nxt("x", 2)
            P.act(lambda e, i=i, j=j: e.activation(out=xb[j][:], in_=h[:, i, :], func=AF.Square,
                                                   accum_out=ssq[:, i:i + 1]),
                  reads=[("h", i)], writes=[("xb", j), ("ssq", i)])
            P.dve(lambda e, i=i: e.tensor_scalar(out=rstd[:, i:i + 1], in0=ssq[:, i:i + 1], scalar1=1.0 / D, scalar2=RMS_EPS,
                                                 op0=ALU.mult, op1=ALU.add),
                  reads=[("ssq", i)], writes=[("rstd", i)])
            P.act(lambda e, i=i: e.activation(out=rstd[:, i:i + 1], in_=rstd[:, i:i + 1], func=AF.Sqrt),
                  reads=[("rstd", i)], writes=[("rstd", i)])
            P.dve(lambda e, i=i: e.reciprocal(out=rstd[:, i:i + 1], in_=rstd[:, i:i + 1]), reads=[("rstd", i)], writes=[("rstd", i)])
        for i in range(nb):
            j = nxt("x", 2)
            P.act(lambda e, i=i, j=j: e.activation(out=xb[j][:], in_=h[:, i, :], func=AF.Copy, scale=rstd[:, i:i + 1]),
                  reads=[("h", i), ("rstd", i)], writes=[("xb", j)])
            for g8 in range(2):
                tb_ = nxt("t", 2)
                for c in range(8):
                    kc = g8 * 8 + c
                    P.pe(lambda e, kc=kc, c=c, j=j, tb_=tb_: e.transpose(
                        psT[tb_][:, c * 128:(c + 1) * 128], xb[j][:, kc * 128:(kc + 1) * 128], identb[:]),
                        reads=[("xb", j), "identb"], writes=[("psT", tb_)])
                P.dve(lambda e, g8=g8, i=i, tb_=tb_: e.tensor_tensor(
                    out=hnT[:, g8 * 8:(g8 + 1) * 8, i * 128:(i + 1) * 128],
                    in0=psT[tb_][:].rearrange("p (c t) -> p c t", t=128),
                    in1=prmT[:, g8 * 8:(g8 + 1) * 8, grow:grow + 1].to_broadcast([128, 8, 128]), op=ALU.mult),
                    reads=[("psT", tb_), "prmT"], writes=[("hnT", i)])

    def out_proj(nb, scr, key, src):
        for ng in range(4):
            wj = load_slab(scr, key, ng)
            for i in range(nb):
                pj = nxt("p", 2)
                for kc in range(16):
                    P.pe(lambda e, kc=kc, i=i, pj=pj, wj=wj: e.matmul(
                        psP[pj][:], lhsT=src[:, kc, i * 128:(i + 1) * 128], rhs=wbuf[wj][:, kc, :],
                        start=(kc == 0), stop=(kc == 15)),
                        reads=[("wbuf", wj), ("ogcz", kc)], writes=[("psP", pj)])
                P.dve(lambda e, i=i, pj=pj, ng=ng: e.tensor_tensor(
                    out=h[:, i, ng * 512:(ng + 1) * 512], in0=h[:, i, ng * 512:(ng + 1) * 512], in1=psP[pj][:], op=ALU.add),
                    reads=[("h", i), ("psP", pj)], writes=[("h", i)])

    def final_norm(nb, dsts):
        for i in range(nb):
            j = nxt("x", 2)
            P.act(lambda e, i=i, j=j: e.activation(out=xb[j][:], in_=h[:, i, :], func=AF.Square,
                                                   accum_out=ssq[:, i:i + 1]),
                  reads=[("h", i)], writes=[("xb", j), ("ssq", i)])
        P.dve(lambda e: e.tensor_scalar(out=rstd[:, 0:nb], in0=ssq[:, 0:nb], scalar1=1.0 / D, scalar2=RMS_EPS,
                                        op0=ALU.mult, op1=ALU.add),
              reads=[("ssq", i) for i in range(nb)], writes=[("rstd", i) for i in range(nb)])
        P.act(lambda e: e.activation(out=rstd[:, 0:nb], in_=rstd[:, 0:nb], func=AF.Sqrt), reads=[("rstd", i) for i in range(nb)], writes=[("rstd", i) for i in range(nb)])
        P.dve(lambda e: e.reciprocal(out=rstd[:, 0:nb], in_=rstd[:, 0:nb]), reads=[("rstd", i) for i in range(nb)], writes=[("rstd", i) for i in range(nb)])
        SGK = [("sg1", cc) for cc in range(16)]
        fgv = sg1[:].rearrange("p a b -> p (a b)")[:, 0:2 * D].bitcast(F32)
        P.dma("sp", fgv, fgd, writes=SGK)
        for i in range(nb):
            if dsts[i] is None:
                continue
            P.dve(lambda e, i=i: e.scalar_tensor_tensor(out=h[:, i, :], in0=h[:, i, :], scalar=rstd[:, i:i + 1],
                                                        in1=fgv, op0=ALU.mult, op1=ALU.mult),
                  reads=[("h", i), ("rstd", i)] + SGK, writes=[("h", i)])
            P.dma("pool", dsts[i], h[:, i, :], reads=[("h", i)])

    def kv_out_unit(wj, hp, i, dst, r0):
        pj = nxt("p", 2)
        wsec = wbuf[wj][:].rearrange("p a b -> p (a b)").rearrange("p (j kc m) -> p j kc m", j=4, kc=16)
        for kc in range(16):
            P.pe(lambda e, kc=kc: e.matmul(psP[pj][:, 0:256].rearrange("p (a b) -> p a b", b=128),
                                           lhsT=hnT[:, kc, i * 128:(i + 1) * 128],
                                           rhs=wsec[:, 1:3, kc, :], start=(kc == 0), stop=(kc == 15)),
                 reads=[("wbuf", wj)] + [("hnT", ii) for ii in range(NB)], writes=[("psP", pj)])
        kj = 0
        P.act(lambda e: e.copy(out=kvo[kj][:], in_=psP[pj][:, 0:256].rearrange("p (a b) -> p a b", b=128)),
              reads=[("psP", pj)], writes=[("kvo", kj)])
        P.dma("pool", dst[:, r0:r0 + 128, hp * 128:(hp + 1) * 128].rearrange("a r c -> r a c"), kvo[kj][:],
              reads=[("kvo", kj)])

    def l0_proj_fillers(hp, T, blocks, smp, par, kind, wj, kvout):
        fl = []

        def fq():
            w_unit(wj, 0, T, lambda ps, k: P.act(
                lambda e: e.copy(out=QT[par][:, 0:T], in_=ps[:, 0:T]), reads=[k], writes=[("QT", par)]))

        def fk():
            def ev(ps, k):
                if smp:
                    so_ = len(blocks) * 128
                    P.act(lambda e: e.copy(out=KTn[par][:], in_=ps[:, so_:so_ + 128]), reads=[k], writes=[("KTn", par)])
                for i in range(len(blocks)):
                    sl = blocks[i] % NSLOT
                    P.dve(lambda e, i=i, sl=sl: e.tensor_copy(out=KT[:, hp, sl, :], in_=ps[:, i * 128:(i + 1) * 128]),
                          reads=[k], writes=[("KT", hp, sl)])
            w_unit(wj, 128, T, ev)

        def fv():
            w_unit(wj, 256, T, lambda ps, k: P.act(
                lambda e: e.copy(out=VT[par][:, 0:T], in_=ps[:, 0:T]), reads=[k], writes=[("VT", par)]))

        def fv2():
            if not blocks:
                return
            r = nxt("t", 2)
            for i in range(len(blocks)):
                P.pe(lambda e, i=i: e.transpose(psT[r][:, i * 128:(i + 1) * 128], VT[par][:, i * 128:(i + 1) * 128], identb[:]),
                     reads=[("VT", par), "identb"], writes=[("psT", r)])
            for i in range(len(blocks)):
                sl = blocks[i] % NSLOT
                P.dve(lambda e, sl=sl, i=i: e.tensor_copy(
                    out=V[:, sl, 2 * hp:2 * hp + 2, 0:HD],
                    in_=psT[r][:, i * 128:(i + 1) * 128].rearrange("p (a b) -> p a b", b=HD)),
                    reads=[("psT", r)], writes=[("V", sl, hp)])

        def fg_():
            def ev(ps, k):
                P.act(lambda e: e.activation(out=tg[par][:, 0:T], in_=ps[:, 0:T], func=AF.Tanh, scale=0.5),
                      reads=[k], writes=[("ah", par)])
                P.dve(lambda e: e.scalar_tensor_tensor(out=sgT[par][:, 0:T], in0=tg[par][:, 0:T], scalar=1.0,
                                                       in1=ps[:, 0:T], op0=ALU.add, op1=ALU.mult),
                      reads=[k, ("ah", par)], writes=[("sgT", par)])
            w_unit(wj, 384, T, ev)

        if kind == "kv":
            fl += [fv, fk, fv2]
        else:
            fl += [fv, fq, fk, fv2, fg_]
        for (i, dst, r0) in kvout:
            fl.append(lambda i=i, dst=dst, r0=r0: kv_out_unit(wj, hp, i, dst, r0))
        return fl

    def att_qk(hp, par, i, b, e_, u):
        hq = slice(64 * e_, 64 * e_ + 64)
        s1 = nxt("s", 3)
        s2 = nxt("s", 3)
        for dl in range(5):
            sl = (b - dl) % NSLOT
            bank = s1 if dl < 2 else s2
            col = dl * 128 if dl < 2 else (dl - 2) * 128
            P.pe(lambda e, sl=sl, bank=bank, col=col: e.matmul(
                psS[bank][:, col:col + 128], lhsT=KT[hq, hp, sl, :], rhs=QT[par][hq, i * 128:(i + 1) * 128],
                start=True, stop=True),
                reads=[("KT", hp, sl), ("QT", par)], writes=[("psS", bank)])
        return (s1, s2)

    def att_exp(hp, e_, u, banks):
        hh = 2 * hp + e_
        s1, s2 = banks
        b1 = ("psS", s2)
        b0 = ("psS", s1)
        pk = ("P", u)
        P.act(lambda e: e.activation(out=et[u][:], in_=psS[s1][:, 0:256].rearrange("p (a b) -> p a b", b=128),
                                     func=AF.Exp, scale=0.125),
              reads=[b0], writes=[("et", u)])
        P.dve(lambda e: e.tensor_tensor(out=Pb[u][:, 0:2, :], in0=et[u][:], in1=Eb[hp % 2][:, :, e_, :], op=ALU.mult),
              reads=[("et", u), ("Eb", hp % 2)], writes=[pk])
        P.act(lambda e: e.activation(out=Pb[u][:, 2:4, :], in_=psS[s2][:, 0:256].rearrange("p (a b) -> p a b", b=128),
                                     func=AF.Exp, bias=cvec[:, hh:hh + 1], scale=0.125),
              reads=[b1, "cvec"], writes=[pk])
        P.act(lambda e: e.activation(out=Pb[u][:, 4, 0:64], in_=psS[s2][:, 256:320],
                                     func=AF.Exp, bias=cvec[:, hh:hh + 1], scale=0.125),
              reads=[b1, "cvec"], writes=[pk])
        P.act(lambda e: e.activation(out=Pb[u][64:128, 4, 64:128], in_=psS[s2][64:128, 320:384],
                                     func=AF.Exp, bias=cvec[64:128, hh:hh + 1], scale=0.125),
              reads=[b1, "cvec"], writes=[pk])

    def att_pv(hp, i, b, e_, u):
        hh = 2 * hp + e_
        for dl in range(5):
            sl = (b - dl) % NSLOT
            P.pe(lambda e, dl=dl, sl=sl: e.matmul(
                psO[:, (i * 2 + e_) * 65:(i * 2 + e_ + 1) * 65], lhsT=Pb[u][:, dl, :], rhs=V[:, sl, hh, :],
                start=(dl == 0), stop=(dl == 4)),
                reads=[("P", u), ("V", sl, hp), ("Vone", sl)], writes=[("psO",)])

    def att_finish(hp, par, i, q0, nq):
        rj = nxt("u", 2)
        ov = psO[0:nq, i * 130:(i + 1) * 130].rearrange("p (a b) -> p a b", b=65)
        P.dve(lambda e: e.tensor_scalar(out=rec[rj][0:nq], in0=ov[:, :, 64:65], scalar1=1e-30, scalar2=None, op0=ALU.add),
              reads=[("psO",)], writes=[("rec", rj)])
        P.dve(lambda e: e.reciprocal(out=rec[rj][0:nq], in_=rec[rj][0:nq]), reads=[("rec", rj)], writes=[("rec", rj)])
        P.dve(lambda e: e.tensor_tensor(out=on[rj][0:nq], in0=ov[:, :, 0:64], in1=rec[rj][0:nq].to_broadcast([nq, 2, HD]),
                                        op=ALU.mult),
              reads=[("psO",), ("rec", rj)], writes=[("on", rj)])

        def part2():
            r = nxt("t", 2)
            P.pe(lambda e: e.transpose(psT[r][:, 0:nq], on[rj][0:nq].rearrange("p a b -> p (a b)"), identb[0:nq, 0:nq]),
                 reads=[("on", rj), "identb"], writes=[("psT", r)])
            P.dve(lambda e: e.tensor_tensor(out=ogcz[:, hp, q0:q0 + nq], in0=psT[r][:, 0:nq],
                                            in1=sgT[par][:, q0:q0 + nq], op=ALU.mult),
                  reads=[("psT", r), ("sgT", par)], writes=[("ogcz", hp)])
        return part2

    def sample_attention(hp, par, fillers, so, pre_done=False, only_pre=False):
        units = [(s, e_) for s in range(4) for e_ in range(2)]
        banks = {}

        def load_a(s):
            j = s % 2
            P.dma("sp", kcf[j][:], ck[s, :, hp * 128:(hp + 1) * 128].rearrange("(a p) c -> p a c", p=128),
                  writes=[("kcf", 0)])
            P.dve(lambda e: e.tensor_copy(out=kcb[j][:], in_=kcf[j][:]), reads=[("kcf", 0)], writes=[("kcb", j)])
            P.dma("sp", vcf[j][:], cv[s, :, hp * 128:(hp + 1) * 128].rearrange("(a p) c -> p a c", p=128),
                  writes=[("vcf", 0)])
            P.pool(lambda e: e.tensor_copy(out=Vc[j][:, :, :, 0:HD], in_=vcf[j][:].rearrange("p a (t d) -> p a t d", d=HD)),
                   reads=[("vcf", 0)], writes=[("Vc", j)])

        if only_pre:
            load_a(0)
            return None

        def load_b(s):
            j = s % 2
            tb_ = nxt("t", 2)
            for a in range(4):
                P.pe(lambda e, a=a: e.transpose(psT[tb_][:, a * 128:(a + 1) * 128], kcb[j][:, a, :], identb[:]),
                     reads=[("kcb", j), "identb"], writes=[("psT", tb_)])
            P.act(lambda e: e.copy(out=KTc[j][:], in_=psT[tb_][:, 0:512]), reads=[("psT", tb_)], writes=[("KTc", j)])
            r = nxt("t", 2)
            P.pe(lambda e: e.transpose(psT[r][0:32, 0:128], VT[par][:, so + s * 32:so + (s + 1) * 32], identb[:]),
                 reads=[("VT", par), "identb"], writes=[("psT", r)])
            P.dve(lambda e: e.tensor_copy(out=Vn[j][:, :, 0:HD], in_=psT[r][0:32, 0:128].rearrange("p (a b) -> p a b", b=HD)),
                  reads=[("psT", r)], writes=[("Vn", j)])

        def qk(ui):
            s, e_ = units[ui]
            j = s % 2
            hq = slice(64 * e_, 64 * e_ + 64)
            q = QT[par][hq, so + s * 32:so + (s + 1) * 32]
            s1 = nxt("s", 3)
            s2 = nxt("s", 3)
            banks[ui] = (s1, s2)
            for a in range(3):
                P.pe(lambda e, a=a: e.matmul(psS[s2][:, a * 32:(a + 1) * 32], lhsT=KTc[j][hq, a * 128:(a + 1) * 128],
                                             rhs=q, start=True, stop=True),
                     reads=[("KTc", j), ("QT", par)], writes=[("psS", s2)])
            P.pe(lambda e: e.matmul(psS[s1][:, 0:32], lhsT=KTc[j][hq, 384:512], rhs=q, start=True, stop=True),
                 reads=[("KTc", j), ("QT", par)], writes=[("psS", s1)])
            P.pe(lambda e: e.matmul(psS[s1][0:32, 32:64], lhsT=KTn[par][hq, s * 32:(s + 1) * 32], rhs=q,
                                    start=True, stop=True),
                 reads=[("KTn", par), ("QT", par)], writes=[("psS", s1)])

        def ex(ui):
            s, e_ = units[ui]
            u = ui % 2
            hh = 2 * hp + e_
            s1, s2 = banks[ui]
            pk = ("P", u)
            P.act(lambda e: e.activation(out=Pb[u][:, 2, 0:96], in_=psS[s2][:, 0:96], func=AF.Exp,
                                         bias=cvec[:, hh:hh + 1], scale=0.125),
                  reads=[("psS", s2), "cvec"], writes=[pk])
            P.act(lambda e: e.activation(out=et[u][:, 0, 0:32], in_=psS[s1][:, 0:32], func=AF.Exp, scale=0.125),
                  reads=[("psS", s1)], writes=[("et", u)])
            P.act(lambda e: e.activation(out=et[u][0:32, 1, 0:32], in_=psS[s1][0:32, 32:64], func=AF.Exp, scale=0.125),
                  reads=[("psS", s1)], writes=[("et", u)])
            P.dve(lambda e: e.tensor_tensor(out=Pb[u][:, 0, 0:32], in0=et[u][:, 0, 0:32], in1=Eb[hp % 2][:, 1, e_, 0:32], op=ALU.mult),
                  reads=[("et", u), ("Eb", hp % 2)], writes=[pk])
            P.dve(lambda e: e.tensor_tensor(out=Pb[u][0:32, 1, 0:32], in0=et[u][0:32, 1, 0:32], in1=Eb[hp % 2][0:32, 0, e_, 0:32], op=ALU.mult),
                  reads=[("et", u), ("Eb", hp % 2)], writes=[pk])

        def pv(ui):
            s, e_ = units[ui]
            u = ui % 2
            j = s % 2
            sr = s % 2
            oreg = psO[0:32, (sr * 2 + e_) * 65:(sr * 2 + e_ + 1) * 65]
            for a in range(3):
                P.pe(lambda e, a=a: e.matmul(oreg, lhsT=Pb[u][:, 2, a * 32:(a + 1) * 32], rhs=Vc[j][:, a, e_, :],
                                             start=(a == 0), stop=False),
                     reads=[("P", u), ("Vc", j)], writes=[("psO",)])
            P.pe(lambda e: e.matmul(oreg, lhsT=Pb[u][:, 0, 0:32], rhs=Vc[j][:, 3, e_, :], start=False, stop=False),
                 reads=[("P", u), ("Vc", j)], writes=[("psO",)])
            P.pe(lambda e: e.matmul(oreg, lhsT=Pb[u][0:32, 1, 0:32], rhs=Vn[j][:, e_, :], start=False, stop=True),
                 reads=[("P", u), ("Vn", j)], writes=[("psO",)])

        if not pre_done:
            load_a(0)
        load_b(0)
        qk(0); ex(0)
        dfr = []
        for ui in range(len(units)):
            s, e_ = units[ui]
            if e_ == 0 and s + 1 < 4:
                load_a(s + 1)
            if e_ == 1 and s + 1 < 4:
                load_b(s + 1)
            if ui + 1 < len(units):
                qk(ui + 1); ex(ui + 1)
            if fillers:
                fillers.pop(0)()
            pv(ui)
            if dfr and e_ == 1:
                dfr.pop(0)()
            if e_ == 1:
                dfr.append(att_finish_sample(hp, par, s, so))
        while fillers:
            fillers.pop(0)()
        while dfr:
            dfr.pop(0)()
        return load_a

    def att_finish_sample(hp, par, s, so):
        rj = nxt("u", 2)
        sr = s % 2
        ov = psO[0:32, sr * 130:(sr + 1) * 130].rearrange("p (a b) -> p a b", b=65)
        P.dve(lambda e: e.tensor_scalar(out=rec[rj][0:32], in0=ov[:, :, 64:65], scalar1=1e-30, scalar2=None, op0=ALU.add),
              reads=[("psO",)], writes=[("rec", rj)])
        P.dve(lambda e: e.reciprocal(out=rec[rj][0:32], in_=rec[rj][0:32]), reads=[("rec", rj)], writes=[("rec", rj)])
        P.dve(lambda e: e.tensor_tensor(out=on[rj][0:32], in0=ov[:, :, 0:64], in1=rec[rj][0:32].to_broadcast([32, 2, HD]),
                                        op=ALU.mult),
              reads=[("psO",), ("rec", rj)], writes=[("on", rj)])
        def part2():
            r = nxt("t", 2)
            P.pe(lambda e: e.transpose(psT[r][:, 0:32], on[rj][0:32].rearrange("p a b -> p (a b)"), identb[0:32, 0:32]),
                 reads=[("on", rj), "identb"], writes=[("psT", r)])
            P.dve(lambda e: e.tensor_tensor(out=ogcz[:, hp, so + s * 32:so + (s + 1) * 32], in0=psT[r][:, 0:32],
                                            in1=sgT[par][:, so + s * 32:so + (s + 1) * 32], op=ALU.mult),
                  reads=[("psT", r), ("sgT", par)], writes=[("ogcz", hp)])
        return part2

    def layer0(kind, blocks, smp, kvout_blocks):
        npb = len(blocks)
        nb = npb + (1 if smp else 0)
        T = nb * 128
        so = npb * 128
        if kind == "kv":
            for hp in range(NHP):
                wj = load_slab(scrA, KA, hp, c0=128, c1=384)
                for f in l0_proj_fillers(hp, T, blocks, False, hp % 2, kind, wj, []):
                    f()
                cv_pump(3, ("VT", hp % 2))
            return
        deferred = []
        for j in range(2):
            P.pool(lambda e, j=j: e.memset(Pb[j][0:64, 4, 64:128], 0.0), writes=[("P", j)])
        wjs = {0: load_slab(scrA, KA, 0)}
        for f in l0_proj_fillers(0, T, blocks, smp, 0, kind, wjs[0], kvout_blocks):
            f()
        for hp in range(NHP):
            par = hp % 2
            fillers = []
            if hp + 1 < NHP:
                wjs[hp + 1] = load_slab(scrA, KA, hp + 1)
                fillers = l0_proj_fillers(hp + 1, T, blocks, smp, (hp + 1) % 2, kind, wjs[hp + 1], kvout_blocks)
            units = [(i, e_) for i in range(npb) for e_ in range(2)]
            if smp:
                sample_attention(hp, par, None, so, only_pre=True)
            if units:
                i0, e0 = units[0]
                bk = att_qk(hp, par, i0, blocks[i0], e0, 0)
                att_exp(hp, e0, 0, bk)
            for ui in range(len(units)):
                i, e_ = units[ui]
                if ui + 1 < len(units):
                    i2, e2 = units[ui + 1]
                    bk = att_qk(hp, par, i2, blocks[i2], e2, (ui + 1) % 2)
                    att_exp(hp, e2, (ui + 1) % 2, bk)
                if fillers:
                    fillers.pop(0)()
                att_pv(hp, i, blocks[i], e_, ui % 2)
                if deferred and e_ == 1:
                    deferred.pop(0)()
                if e_ == 1:
                    deferred.append(att_finish(hp, par, i, i * 128, 128))
            if smp:
                while deferred:
                    deferred.pop(0)()
                sample_attention(hp, par, fillers, so, pre_done=True)
            while fillers:
                fillers.pop(0)()
            if len(deferred) > 1:
                deferred.pop(0)()
            cv_pump(3, ("VT", (hp + 1) % 2))
        while deferred:
            deferred.pop(0)()
        cv_pump(len(cvq), ("VT", 0))

    def layer1(npb, smp, mask_first):
        nb = npb + (1 if smp else 0)
        T = nb * 128
        Tp = npb * 128
        pw = (CK + Tp) if npb else 0
        sw = CK + 32
        used = pw + (4 * sw if smp else 0)
        pend = []
        ccvS = KTf[:, 0:4096].bitcast(F32)
        CCK = [("KT", hp, sl) for hp in range(5) for sl in range(NSLOT)]

        def stats(cc):
            P.pe(lambda e: e.matmul(psS[0][:, 0:T], lhsT=ones_bf[:], rhs=ogcz[:, cc, 0:T], start=(cc == 0), stop=(cc == 15)),
                 reads=[("ogcz", cc), "ones"], writes=[("psS", 0)])
            jc = cc % 2
            P.pe(lambda e: e.matmul(psS[1][:, 0:T], lhsT=ones_bf[:], rhs=c2[jc][:, 0:T], start=(cc == 0), stop=(cc == 15)),
                 reads=[("c2", jc), "ones"], writes=[("psS", 1)])

        convq = []

        def conv_unit(cc, j):
            pj = nxt("p", 2)
            if npb:
                for k in range(CW):
                    P.pe(lambda e, k=k: e.matmul(psP[pj][:, 0:Tp], lhsT=dgb[0][:, k, :], rhs=ubb[j][:, k:k + Tp],
                                                 start=(k == 0), stop=(k == CW - 1)),
                         reads=[("dgb", 0), ("ubb", j)] + L0KEYS, writes=[("psP", pj)])
            if smp:
                ubs_b = ubb[j][:, pw:pw + 4 * sw].rearrange("p (s w) -> p s w", w=sw)
                outv = psP[pj][:, Tp:T].rearrange("p (s w) -> p s w", w=32)
                for k in range(CW):
                    P.pe(lambda e, k=k: e.matmul(outv, lhsT=dgb[0][:, k, :], rhs=ubs_b[:, :, k:k + 32],
                                                 start=(k == 0), stop=(k == CW - 1)),
                         reads=[("dgb", 0), ("ubb", j)] + L0KEYS, writes=[("psP", pj)])
            P.act(lambda e: e.activation(out=ogcz[:, cc, 0:T], in_=psP[pj][:, 0:T], func=AF.Identity,
                                         bias=prmT[:, cc, 31:32], scale=1.0),
                  reads=[("psP", pj), "prmT"], writes=[("ogcz", cc)])
            P.act(lambda e: e.activation(out=c2[j][:, 0:T], in_=psP[pj][:, 0:T], func=AF.Square,
                                         bias=prmT[:, cc, 31:32], scale=1.0),
                  reads=[("psP", pj), "prmT"], writes=[("c2", j)])
            pend.append(cc)
            if len(pend) > 1:
                stats(pend.pop(0))

        if smp:
            P.dma("sp", ccvS[0:4 * CK, :], ccv, writes=CCK)
        for cc in range(16):
            wj = load_slab(scrB, KB, cc, width=384)
            j = cc % 2
            ubs = ub[j][:, pw:pw + 4 * sw].rearrange("p (s w) -> p s w", w=sw) if smp else None
            if smp:
                P.pe(lambda e, cc=cc: e.matmul(psS[2][:, 0:4 * CK], lhsT=ccvS[0:4 * CK, cc * 128:(cc + 1) * 128],
                                               rhs=identf[0:4 * CK, 0:4 * CK], start=True, stop=True),
                     reads=CCK + ["identf"], writes=[("psS", 2)])
                P.act(lambda e, ubs=ubs: e.copy(out=ubs[:, :, 0:CK], in_=psS[2][:, 0:4 * CK].rearrange("p (s w) -> p s w", w=CK)),
                      reads=[("psS", 2)], writes=[("ub", 0)])
            if npb:
                P.act(lambda e, cc=cc, j=j: e.copy(out=ub[j][:, 0:CK], in_=ust[:, cc, :]),
                      reads=[("ust", cc)], writes=[("ub", 0)])
            def ev_a(ps, k, cc=cc, j=j):
                P.act(lambda e: e.activation(out=ah[j][:, 0:T], in_=ps[:, 0:T], func=AF.Identity,
                                             bias=prmH[:, cc, 2:3], scale=0.5),
                      reads=[k, "prmH"], writes=[("ah", j)])
            w_unit(wj, 0, T, ev_a)

            def ev_b(ps, k, cc=cc, j=j, ubs=ubs):
                P.act(lambda e: e.activation(out=tb[j][:, 0:T], in_=ps[:, 0:T], func=AF.Tanh,
                                             bias=prmH[:, cc, 3:4], scale=0.5),
                      reads=[k, "prmH"], writes=[("tb", j)])
                if npb:
                    P.dve(lambda e: e.scalar_tensor_tensor(
                        out=ub[j][:, CK:CK + Tp], in0=tb[j][:, 0:Tp], scalar=1.0, in1=ah[j][:, 0:Tp],
                        op0=ALU.add, op1=ALU.mult),
                        reads=[("tb", j), ("ah", j)], writes=[("ub", 0)])
                if smp:
                    P.dve(lambda e: e.scalar_tensor_tensor(
                        out=ubs[:, :, CK:CK + 32], in0=tb[j][:, Tp:T].rearrange("p (s w) -> p s w", w=32), scalar=1.0,
                        in1=ah[j][:, Tp:T].rearrange("p (s w) -> p s w", w=32), op0=ALU.add, op1=ALU.mult),
                        reads=[("tb", j), ("ah", j)], writes=[("ub", 0)])
                if mask_first:
                    P.dve(lambda e: e.tensor_scalar(out=ub[j][:, CK:CK + 128], in0=ub[j][:, CK:CK + 128],
                                                    scalar1=hmask[:, 0:1], scalar2=None, op0=ALU.mult),
                          reads=[("ub", 0), "hmask"], writes=[("ub", 0)])
            w_unit(wj, 128, T, ev_b)

            def ev_g(ps, k, cc=cc, j=j):
                P.act(lambda e: e.activation(out=vh[j][:, 0:T], in_=ps[:, 0:T], func=AF.Identity,
                                             bias=prmH[:, cc, 4:5], scale=0.5),
                      reads=[k, "prmH"], writes=[("vh", j)])
                P.act(lambda e: e.activation(out=tt[j][:, 0:T], in_=ps[:, 0:T], func=AF.Tanh,
                                             bias=prmH[:, cc, 4:5], scale=0.5),
                      reads=[k, "prmH"], writes=[("tt", j)])
                P.pool(lambda e: e.tensor_tensor(out=tt[j][:, 0:T], in0=tt[j][:, 0:T], in1=vh[j][:, 0:T], op=ALU.mult),
                       reads=[("tt", j), ("vh", j)], writes=[("tt", j)])
                P.pool(lambda e: e.tensor_tensor(out=sg1[:, cc, 0:T], in0=tt[j][:, 0:T], in1=vh[j][:, 0:T], op=ALU.add),
                       reads=[("tt", j), ("vh", j)], writes=[("sg1", cc)])
            w_unit(wj, 256, T, ev_g)
            if convq:
                conv_unit(*convq.pop(0))
            P.act(lambda e, j=j: e.copy(out=ubb[j][:, 0:used], in_=ub[j][:, 0:used]),
                  reads=[("ub", 0)], writes=[("ubb", j)])
            P.dve(lambda e, cc=cc: e.tensor_tensor(
                out=dgb[0][:], in0=identb[:].unsqueeze(1).to_broadcast([128, CW, 128]),
                in1=prmT[:, cc, 0:CW].unsqueeze(2).to_broadcast([128, CW, 128]), op=ALU.mult),
                reads=["identb", "prmT"], writes=[("dgb", 0)] + L0KEYS)
            convq.append((cc, j))
            if smp:
                for s_ in range(4):
                    P.pe(lambda e, s_=s_, ubs=ubs: e.matmul(psS[2][0:CK, s_ * 128:(s_ + 1) * 128], lhsT=ubs[:, s_, 32:32 + CK],
                                                            rhs=identf[:], start=True, stop=True),
                         reads=[("ub", 0), "identf"], writes=[("psS", 2)])
                P.act(lambda e: e.copy(out=cst[0:CK], in_=psS[2][0:CK, 0:512].rearrange("p (s c) -> p s c", c=128)),
                      reads=[("psS", 2)], writes=["cst"])
                P.dma("pool", csd[:, :, cc * 128:(cc + 1) * 128].rearrange("s r c -> r s c"), cst[0:CK], reads=["cst"])
            if npb:
                P.act(lambda e, cc=cc, j=j: e.copy(out=ust[:, cc, :], in_=ub[j][:, Tp:Tp + CK]),
                      reads=[("ub", 0)], writes=[("ust", cc)])
        while convq:
            conv_unit(*convq.pop(0))
        while pend:
            stats(pend.pop(0))
        P.dve(lambda e, j=j: e.tensor_copy(out=mu[:, 0:T], in_=psS[0][:, 0:T]), reads=[("psS", 0)], writes=["mu"])
        P.dve(lambda e, j=j: e.tensor_tensor(out=rsd[:, 0:T], in0=mu[:, 0:T], in1=mu[:, 0:T], op=ALU.mult), reads=["mu"], writes=["rsd"])
        P.dve(lambda e, j=j: e.tensor_tensor(out=rsd[:, 0:T], in0=psS[1][:, 0:T], in1=rsd[:, 0:T], op=ALU.subtract),
              reads=[("psS", 1), "rsd"], writes=["rsd"])
        P.dve(lambda e, j=j: e.tensor_scalar(out=rsd[:, 0:T], in0=rsd[:, 0:T], scalar1=LN_EPS, scalar2=None, op0=ALU.add),
              reads=["rsd"], writes=["rsd"])
        P.act(lambda e, j=j: e.activation(out=rsd[:, 0:T], in_=rsd[:, 0:T], func=AF.Sqrt), reads=["rsd"], writes=["rsd"])
        P.dve(lambda e, j=j: e.reciprocal(out=rsd[:, 0:T], in_=rsd[:, 0:T]), reads=["rsd"], writes=["rsd"])
        T1 = [(ah[0], ("ah", 0)), (ah[1], ("ah", 1))]
        TT = [(tt[0], ("tt", 0)), (tt[1], ("tt", 1))]
        VH = [(vh[0], ("vh", 0)), (vh[1], ("vh", 1))]
        def z_front(cc):
            t1b, t1k = T1[cc % 2]
            ttb, ttk = TT[cc % 2]
            vhb, vhk = VH[cc % 2]
            P.dve(lambda e: e.tensor_tensor(out=t1b[:, 0:T], in0=ogcz[:, cc, 0:T], in1=mu[:, 0:T], op=ALU.subtract),
                  reads=[("ogcz", cc), "mu"], writes=[t1k])
            P.dve(lambda e: e.tensor_tensor(out=t1b[:, 0:T], in0=t1b[:, 0:T], in1=rsd[:, 0:T], op=ALU.mult),
                  reads=[t1k, "rsd"], writes=[t1k])
            P.act(lambda e: e.activation(out=ttb[:, 0:T], in_=t1b[:, 0:T], func=AF.Tanh,
                                         bias=prmH[:, cc, 1:2], scale=prmH[:, cc, 0:1]),
                  reads=[t1k, "prmH"], writes=[ttk])
            P.act(lambda e: e.activation(out=vhb[:, 0:T], in_=t1b[:, 0:T], func=AF.Identity,
                                         bias=prmH[:, cc, 1:2], scale=prmH[:, cc, 0:1]),
                  reads=[t1k, "prmH"], writes=[vhk])

        def z_back(cc):
            ttb, ttk = TT[cc % 2]
            vhb, vhk = VH[cc % 2]
            P.dve(lambda e: e.scalar_tensor_tensor(out=vhb[:, 0:T], in0=ttb[:, 0:T], scalar=1.0, in1=vhb[:, 0:T],
                                                   op0=ALU.add, op1=ALU.mult),
                  reads=[ttk, vhk], writes=[vhk])
            P.pool(lambda e: e.tensor_tensor(out=ogcz[:, cc, 0:T], in0=vhb[:, 0:T], in1=sg1[:, cc, 0:T], op=ALU.mult),
                   reads=[vhk, ("sg1", cc)], writes=[("ogcz", cc)])

        z_front(0)
        for cc in range(16):
            if cc + 1 < 16:
                z_front(cc + 1)
            z_back(cc)

    ones_bf = P.sb("ones_bf", [128, 128], BF16)
    P.pool(lambda e: e.memset(ones_bf[:], 1.0 / D), writes=["ones"])

    tiles = []
    for b in range(0, 4, NB):
        tiles.append(("kv", list(range(b, min(b + NB, 4))), False))
    full = [list(range(b, min(b + NB, NPB))) for b in range(4, NPB, NB)]
    if max_full_tiles is not None:
        full = full[:max_full_tiles]
    if len(full) and len(full[-1]) < NB and max_full_tiles is None:
        for bl in full[:-1]:
            tiles.append(("full", bl, False))
        tiles.append(("full", full[-1], True))
    else:
        for bl in full:
            tiles.append(("full", bl, False))
        tiles.append(("full", [], True))

    for (kind, blocks, smp) in tiles:
        if kind == "kv":
            slab_plan.extend([("A", hp, 128, 384) for hp in range(NHP)])
        else:
            slab_plan.extend([("A", hp, 0, 512) for hp in range(NHP)] + [("O", g, 0, 512) for g in range(4)]
                             + [("B", cc, 0, 384) for cc in range(16)] + [("P", g, 0, 512) for g in range(4)])

    for (kind, blocks, smp) in tiles:
        npb = len(blocks)
        nb = npb + (1 if smp else 0)
        T = nb * 128
        for i, b in enumerate(blocks):
            P.dma("sp", h[:, i, :], xp[b * 128:(b + 1) * 128, :], writes=[("h", i)])
        if smp:
            P.dma("sp", h[:, npb, :], xs, writes=[("h", npb)])
        for b in blocks:
            sl = b % NSLOT
            keys = [("V", sl, hp) for hp in range(NHP)] + [("Vone", sl)]
            if b < HALO_BLKS:
                P.pool(lambda e, sl=sl: e.tensor_scalar(out=V[:, sl, :, HD:HD + 1], in0=twos[:], scalar1=hmask[:, 0:1],
                                                        scalar2=None, op0=ALU.mult),
                       reads=["twos", "hmask"], writes=keys)
            else:
                P.pool(lambda e, sl=sl: e.tensor_copy(out=V[:, sl, :, HD:HD + 1], in_=twos[:]), reads=["twos"], writes=keys)
        rmsnorm_T(nb, 37)
        kvout = []
        if kind == "full":
            for i, b in enumerate(blocks):
                ob = b - HALO_BLKS
                if ob >= 12:
                    kvout.append((i, kvp, (ob - 12) * 128))
            if smp:
                kvout.append((npb, kvs, 0))
        layer0(kind, blocks, smp, kvout)
        if kind == "kv":
            continue
        out_proj(nb, scrO, KO, ogcz)
        rmsnorm_T(nb, 38)
        layer1(npb, smp, mask_first=(npb > 0 and blocks[0] == 4))
        out_proj(nb, scrP, KP, ogcz)
        dsts = [(yp[(b - HALO_BLKS) * 128:(b - HALO_BLKS + 1) * 128, :] if b >= HALO_BLKS else None) for b in blocks]
        if smp:
            dsts.append(ysd)
        final_norm(nb, dsts)
        if npb and blocks[-1] == NPB - 1:
            for g4 in range(4):
                for c in range(4):
                    cc = g4 * 4 + c
                    P.pe(lambda e, cc=cc, c=c: e.matmul(psS[2][0:CK, c * 128:(c + 1) * 128], lhsT=ust[:, cc, :], rhs=identf[:],
                                                        start=True, stop=True),
                         reads=[("ust", cc), "identf"], writes=[("psS", 2)])
                P.act(lambda e: e.copy(out=cst[0:CK], in_=psS[2][0:CK, 0:512].rearrange("p (s c) -> p s c", c=128)),
                      reads=[("psS", 2)], writes=["cst"])
                P.dma("pool", cpd[:, g4 * 512:(g4 + 1) * 512].rearrange("r (s c) -> r s c", c=128), cst[0:CK], reads=["cst"])

    P.emit()
    return nc, P


_CACHE = {}


def _get_program():
    if "nc" not in _CACHE:
        _CACHE["nc"], _CACHE["P"] = build_program()
    return _CACHE["nc"]


def prepare(x_prompt, x_sample, cache_attn_k, cache_attn_v, cache_conv, norm_g, final_g,
            attn_w_in, attn_w_out, attn_rel_bias, conv_w_in, conv_b_in, conv_w_dw, conv_b_dw,
            conv_ln_g, conv_ln_b, conv_w_out):
    f = lambda a: np.ascontiguousarray(np.asarray(a, dtype=np.float32))
    x_prompt, x_sample = f(x_prompt), f(x_sample)
    ck_all = f(cache_attn_k).reshape(32, 512, D)
    cv_all = f(cache_attn_v).reshape(32, 512, D)
    cc_all = f(cache_conv).reshape(32, CK, D)
    w_ai, w_ao = f(attn_w_in)[0], f(attn_w_out)[0]
    w_ci, w_co = f(conv_w_in)[0], f(conv_w_out)[0]
    table = f(attn_rel_bias)[0]
    prm = np.concatenate([f(conv_w_dw)[0], f(conv_b_dw), f(conv_ln_g), f(conv_ln_b),
                          f(conv_b_in)[0].reshape(3, D), f(norm_g)], axis=0)
    fg = np.ascontiguousarray(np.broadcast_to(f(final_g)[None, :], (128, D)))
    kk = np.arange(128)[:, None]
    qq = np.arange(128)[None, :]
    relT = np.empty((128, 2, NH, 128), np.float32)
    for dl in range(2):
        idx = np.clip(128 * dl + qq - kk, -128, 128) + 128
        relT[:, dl] = np.transpose(table[idx], (0, 2, 1))
    cvec = np.ascontiguousarray(np.broadcast_to(table[256][None, :], (128, NH)))
    ident = np.eye(128, dtype=np.float32)
    xpad = np.concatenate([np.zeros((HALO_BLKS * 128, D), np.float32), x_prompt[0]], axis=0)
    in_maps = []
    for c in range(NCORES):
        s0 = c * OWN
        in_maps.append(dict(
            xp=np.ascontiguousarray(xpad[s0:s0 + NPB * 128]),
            xs=np.ascontiguousarray(x_sample[4 * c:4 * c + 4].reshape(128, D)),
            ck=np.ascontiguousarray(ck_all[4 * c:4 * c + 4]),
            cv=np.ascontiguousarray(cv_all[4 * c:4 * c + 4]),
            ccv=np.ascontiguousarray(cc_all[4 * c:4 * c + 4].reshape(4 * CK, D)),
            w_ai=w_ai, w_ao=w_ao, w_ci=w_ci, w_co=w_co, prm=prm, fg=fg, relT=relT, cvec=cvec, ident=ident,
            hmask=np.full((128, 1), 0.0 if c == 0 else 1.0, np.float32),
        ))
    return in_maps


def assemble(R):
    g = lambda c, k: np.asarray(R[c][k], dtype=np.float32)
    y_prompt = np.concatenate([g(c, "yp") for c in range(NCORES)], axis=0)[None]
    y_sample = np.concatenate([g(c, "ys").reshape(4, 32, D) for c in range(NCORES)], axis=0)
    kvp = g(NCORES - 1, "kvp")
    k_p = kvp[0].reshape(1, 1, 512, NH, HD)
    v_p = kvp[1].reshape(1, 1, 512, NH, HD)
    c_p = g(NCORES - 1, "cp").reshape(1, 1, CK, D)
    ks, vs, cs = [], [], []
    for c in range(NCORES):
        kvn = g(c, "kvs")
        ks.append(np.concatenate([g(c, "kso"), kvn[0].reshape(4, 32, D)], axis=1))
        vs.append(np.concatenate([g(c, "vso"), kvn[1].reshape(4, 32, D)], axis=1))
        cs.append(g(c, "cs"))
    k_s = np.concatenate(ks, axis=0).reshape(1, 32, 512, NH, HD)
    v_s = np.concatenate(vs, axis=0).reshape(1, 32, 512, NH, HD)
    c_s = np.concatenate(cs, axis=0).reshape(1, 32, CK, D)
    return (y_prompt, y_sample, k_p, v_p, c_p, k_s, v_s, c_s)


def kernel(**inputs):
    in_maps = prepare(**inputs)
    nc = _get_program()
    res = run_bass_kernel_spmd(nc, in_maps, core_ids=list(range(NCORES)))
    return assemble(res.results)
```

```python
import contextlib
import numpy as np
import concourse.bass as bass
import concourse.mybir as mybir
from concourse.bass_utils import run_bass_kernel_spmd

F32 = mybir.dt.float32
BF16 = mybir.dt.bfloat16
AF = mybir.ActivationFunctionType
ALU = mybir.AluOpType

ENGS = ("pe", "act", "dve", "pool", "sp")
SEM_CAP = 12000
N_DMA_SEMS = 12

NCORES = 8
D = 2048
NH = 32
HD = 64
NHP = 16
OWN = 2048
HALO_BLKS = 5
NPB = HALO_BLKS + OWN // 128
NB = 3
NSLOT = NB + 4
TMAX = NB * 128
CW = 31
CK = 30
NPRM = 39
RMS_EPS = 1e-6
LN_EPS = 1e-5


class Op:
    __slots__ = ("eng", "fn", "dma", "deps", "flag", "tok", "idx", "pre_wait")

    def __init__(self, eng, fn, dma):
        self.eng = eng
        self.fn = fn
        self.dma = dma
        self.deps = []
        self.flag = False
        self.tok = None
        self.pre_wait = None


class Prog:
    def __init__(self, nc):
        self.nc = nc
        self.ops = []
        self.lastw = {}
        self.readers = {}
        self.stack = contextlib.ExitStack()

    def sb(self, name, shape, dt):
        return self.stack.enter_context(self.nc.sbuf_tensor("s_" + name, list(shape), dt))

    def ps(self, name, shape, dt=F32):
        return self.stack.enter_context(self.nc.psum_tensor("p_" + name, list(shape), dt))

    def op(self, eng, fn, reads=(), writes=(), dma=False):
        o = Op(eng, fn, dma)
        o.idx = len(self.ops)
        psk = [r for r in reads if isinstance(r, tuple) and str(r[0]).startswith("ps")]
        if psk:
            reads = [r for r in reads if r not in psk]
            writes = list(writes) + psk
        deps = {}
        for r in reads:
            w = self.lastw.get(r)
            if w is not None:
                deps[w.idx] = (w, True)
        for r in writes:
            w = self.lastw.get(r)
            if w is not None and w.idx not in deps:
                deps[w.idx] = (w, False)
            for rd in self.readers.get(r, ()):
                if rd.idx not in deps:
                    deps[rd.idx] = (rd, False)
        for p, raw in deps.values():
            if p is o:
                continue
            if p.eng == o.eng and not p.dma and not o.dma:
                if o.eng == "pe":
                    continue
            p.flag = True
            o.deps.append(p)
        for r in reads:
            self.readers.setdefault(r, []).append(o)
        for r in writes:
            self.lastw[r] = o
            self.readers[r] = []
        self.ops.append(o)
        return o

    def pe(self, fn, reads=(), writes=()):
        return self.op("pe", fn, reads, writes)

    def act(self, fn, reads=(), writes=()):
        return self.op("act", fn, reads, writes)

    def dve(self, fn, reads=(), writes=()):
        return self.op("dve", fn, reads, writes)

    def pool(self, fn, reads=(), writes=()):
        return self.op("pool", fn, reads, writes)

    def dma(self, q, out, in_, reads=(), writes=()):
        return self.op(q, lambda e: e.dma_start(out=out, in_=in_), reads, writes, dma=True)

    def emit(self):
        nc = self.nc
        cnt = {e: 0 for e in ENGS}
        dcnt = {e: 0 for e in ENGS}
        n_csem = {e: 0 for e in ENGS}
        for o in self.ops:
            if o.dma:
                k = dcnt[o.eng]
                dcnt[o.eng] += 1
                slot = k % N_DMA_SEMS
                use = k // N_DMA_SEMS
                o.tok = (("d", o.eng, slot), 16 * (use + 1))
                if use > 0:
                    o.pre_wait = (("d", o.eng, slot), 16 * use)
                o.flag = True
            elif o.flag:
                n = cnt[o.eng]
                cnt[o.eng] += 1
                o.tok = (("c", o.eng, n // SEM_CAP), n % SEM_CAP + 1)
                n_csem[o.eng] = n // SEM_CAP + 1
        sems = {}
        for e in ENGS:
            for i in range(n_csem[e]):
                sems[("c", e, i)] = self.stack.enter_context(nc.semaphore(f"c_{e}_{i}"))
            for s in range(min(N_DMA_SEMS, dcnt[e])):
                sems[("d", e, s)] = self.stack.enter_context(nc.semaphore(f"d_{e}_{s}"))
        final = {}
        for o in self.ops:
            if o.dma:
                final[o.tok[0]] = max(final.get(o.tok[0], 0), o.tok[1])
        block = self.stack.enter_context(nc.Block())
        ops = self.ops
        stats = {}

        def run(engname, eng):
            waited = {}
            nw = 0
            mine = [o for o in ops if o.eng == engname]
            for o in mine:
                reqs = []
                if o.pre_wait is not None:
                    reqs.append(o.pre_wait)
                for p in o.deps:
                    reqs.append(p.tok)
                need = {}
                for (sk, v) in reqs:
                    if v > need.get(sk, 0):
                        need[sk] = v
                for sk, v in need.items():
                    if waited.get(sk, 0) >= v:
                        continue
                    waited[sk] = v
                    eng.wait_ge(sems[sk], v)
                    nw += 1
                ins = o.fn(eng)
                if o.flag:
                    ins.then_inc(sems[o.tok[0]], 16 if o.dma else 1)
            for sk, v in final.items():
                if sk[1] == engname and waited.get(sk, 0) < v:
                    eng.wait_ge(sems[sk], v)
            stats[engname] = (len(mine), nw)

        block.sync(lambda e: run("sp", e))
        block.scalar(lambda e: run("act", e))
        block.vector(lambda e: run("dve", e))
        block.gpsimd(lambda e: run("pool", e))
        block.tensor(lambda e: run("pe", e))
        self.stats = stats
        self.stack.close()
        return nc


def build_program(max_full_tiles=None):
    nc = bass.Bass("TRN2", target_bir_lowering=False)
    P = Prog(nc)

    def din(name, shape, dt=F32):
        return nc.dram_tensor(name, list(shape), dt, kind="ExternalInput").ap()

    def dout(name, shape, dt=F32):
        return nc.dram_tensor(name, list(shape), dt, kind="ExternalOutput").ap()

    def dscr(name, shape, dt=BF16):
        return nc.dram_tensor(name, list(shape), dt, kind="Internal").ap()

    xp = din("xp", [NPB * 128, D])
    xs = din("xs", [128, D])
    ck = din("ck", [4, 512, D])
    cv = din("cv", [4, 512, D])
    ccv = din("ccv", [4 * CK, D])
    w_ai = din("w_ai", [D, 4 * D])
    w_ao = din("w_ao", [D, D])
    w_ci = din("w_ci", [D, 3 * D])
    w_co = din("w_co", [D, D])
    prm = din("prm", [NPRM, D])
    fgd = din("fg", [128, D])
    relT = din("relT", [128, 2, NH, 128])
    cvd = din("cvec", [128, NH])
    idd = din("ident", [128, 128])
    hmd = din("hmask", [128, 1])

    yp = dout("yp", [OWN, D])
    ysd = dout("ys", [128, D])
    kvp = dout("kvp", [2, 512, D])
    kvs = dout("kvs", [2, 128, D])
    cpd = dout("cp", [CK, D])
    csd = dout("cs", [4, CK, D])
    kso = dout("kso", [4, 480, D])
    vso = dout("vso", [4, 480, D])

    scrA = dscr("scrA", [NHP, 128, 4, 16, 128])
    scrO = dscr("scrO", [4, 128, 16, 512])
    scrB = dscr("scrB", [16, 128, 3, 16, 128])
    scrP = dscr("scrP", [4, 128, 16, 512])
    scrE = dscr("scrE", [NHP, 128, 2, 2, 128])

    identf = P.sb("identf", [128, 128], F32)
    identb = P.sb("identb", [128, 128], BF16)
    prmT = P.sb("prmT", [128, 16, NPRM], F32)
    prmH = P.sb("prmH", [128, 16, 5], F32)
    cvec = P.sb("cvec", [128, NH], F32)
    hmask = P.sb("hmask", [128, 1], F32)
    twos = P.sb("twos", [128, NH, 1], BF16)
    fdum = P.sb("fdum", [128, 8], F32)
    Eb = [P.sb(f"Eb{j}", [128, 2, 2, 128], BF16) for j in range(2)]
    h = P.sb("h", [128, NB, D], F32)
    xb = [P.sb(f"xb{j}", [128, D], BF16) for j in range(2)]
    ssq = P.sb("ssq", [128, NB], F32)
    rstd = P.sb("rstd", [128, NB], F32)
    hnT = P.sb("hnT", [128, 16, TMAX], BF16)
    KT = P.sb("KT", [128, NHP, NSLOT, 128], BF16)
    V = P.sb("V", [128, NSLOT, NH, HD + 1], BF16)
    L0N = 6 * TMAX + 2048 + 256
    l0t = P.sb("l0t", [128, max(L0N, CW * 128)], BF16)
    _o = {"o": 0}

    def carve(n):
        v = l0t[:, _o["o"]:_o["o"] + n]
        _o["o"] += n
        return v
    QT = [carve(TMAX) for j in range(2)]
    VT = [carve(TMAX) for j in range(2)]
    sgT = [carve(TMAX) for j in range(2)]
    et = [carve(256).rearrange("p (a b) -> p a b", b=128) for j in range(2)]
    Pb = [carve(640).rearrange("p (a b) -> p a b", b=128) for j in range(2)]
    on = [carve(128).rearrange("p (a b) -> p a b", b=HD) for j in range(2)]
    KTn = [carve(128) for j in range(2)]
    dgb = [l0t[:, 0:CW * 128].rearrange("p (a b) -> p a b", b=128)]
    L0KEYS = [(nm, j) for nm in ("QT", "KTn", "VT", "sgT", "et", "P", "on") for j in range(2)]
    rec = [P.sb(f"rec{j}", [128, 2, 1], F32) for j in range(2)]
    ogcz = P.sb("ogcz", [128, 16, TMAX], BF16)
    NW = 2
    wbuf = [P.sb(f"wbuf{j}", [128, 16, 512], BF16) for j in range(NW)]
    kvo = [P.sb("kvo0", [128, 2, 128], F32)] * 2
    UW = max(CK + TMAX, CK + (NB - 1) * 128 + 4 * (CK + 32))
    ub = [P.sb("ub0", [128, UW], F32)] * 2
    ust = P.sb("ust", [128, 16, CK], F32)
    ah = [P.sb(f"ah{j}", [128, TMAX], F32) for j in range(2)]
    tg = ah
    tb = [P.sb(f"tb{j}", [128, TMAX], F32) for j in range(2)]
    ubb = [P.sb(f"ubb{j}", [128, UW], BF16) for j in range(2)]
    c2 = [P.sb(f"c2_{j}", [128, TMAX], BF16) for j in range(2)]
    sg1 = P.sb("sg1", [128, 16, TMAX], BF16)
    mu = P.sb("mu", [128, TMAX], F32)
    rsd = P.sb("rsd", [128, TMAX], F32)
    t1 = ah
    tt = [P.sb(f"tt_{j}", [128, TMAX], F32) for j in range(2)]
    vh = [P.sb(f"vh_{j}", [128, TMAX], F32) for j in range(2)]
    cst = P.sb("cst", [32, 4, 128], F32)
    kcf = [P.sb("kcf0", [128, 4, 128], F32)] * 2
    kcb = [P.sb(f"kcb{j}", [128, 4, 128], BF16) for j in range(2)]
    KTc = [P.sb(f"KTc{j}", [128, 512], BF16) for j in range(2)]
    vcf = [P.sb("vcf0", [128, 4, 128], F32)] * 2
    Vc = [P.sb(f"Vc{j}", [128, 4, 2, HD + 1], BF16) for j in range(2)]
    Vn = [P.sb(f"Vn{j}", [32, 2, HD + 1], BF16) for j in range(2)]

    psS = [P.ps(f"psS{j}", [128, 512], F32) for j in range(3)]
    psP = [P.ps(f"psP{j}", [128, 512], F32) for j in range(2)]
    psO = P.ps("psO", [128, 512], F32)
    psT = [P.ps(f"psT{j}", [128, 1024], BF16) for j in range(2)]

    HS = lambda j: [("hstg", j, a) for a in range(4)]
    ctr = {"w": 0, "p": 0, "t": 0, "x": 0, "kvo": 0, "u": 0, "s": 0}

    def nxt(k, n):
        v = ctr[k]
        ctr[k] = v + 1
        return v % n

    P.dma("sp", identf[:], idd, writes=["identf"])
    P.dma("sp", cvec[:], cvd, writes=["cvec"])
    P.dma("sp", hmask[:], hmd, writes=["hmask"])
    P.dve(lambda e: e.tensor_copy(out=identb[:], in_=identf[:]), reads=["identf"], writes=["identb"])
    P.pool(lambda e: e.memset(twos[:], 2.0), writes=["twos"])
    P.pool(lambda e: e.memset(fdum[:], 0.0), writes=["fdum"])
    for j in range(2):
        P.pool(lambda e, j=j: e.memset(Pb[j][:], 0.0), writes=[("P", j)])
    P.pool(lambda e: e.memset(ust[:], 0.0), writes=[("ust", cc) for cc in range(16)])
    for j in range(2):
        P.pool(lambda e, j=j: e.memset(Vc[j][:], 2.0), writes=[("Vc", j)])
        P.pool(lambda e, j=j: e.memset(Vn[j][:], 2.0), writes=[("Vn", j)])
    P.dma("sp", h[0:NPRM, 0, :], prm, writes=HS(0))
    for g in range(2):
        for c8 in range(8):
            cc = g * 8 + c8
            P.pe(lambda e, cc=cc, c8=c8: e.matmul(
                psS[0][:, c8 * NPRM:(c8 + 1) * NPRM], lhsT=h[0:NPRM, 0, cc * 128:(cc + 1) * 128],
                rhs=identf[0:NPRM, 0:NPRM], start=True, stop=True),
                reads=HS(0) + [("h", 0), "identf"], writes=[("psS", 0)])
        P.dve(lambda e, g=g: e.tensor_copy(
            out=prmT[:, g * 8:(g + 1) * 8, :],
            in_=psS[0][:, 0:8 * NPRM].rearrange("p (c r) -> p c r", r=NPRM)),
            reads=[("psS", 0)], writes=["prmT"])
    P.act(lambda e: e.mul(out=prmH[:], in_=prmT[:, :, 32:37], mul=0.5), reads=["prmT"], writes=["prmH"])
    for dl in range(2):
        for hg in range(4):
            j = (dl * 4 + hg) % 2
            stg = h[:, j, 0:1024].rearrange("p (a b) -> p a b", b=128)
            xv = xb[j][:, 0:1024].rearrange("p (a b) -> p a b", b=128)
            P.dma("sp", stg, relT[:, dl, hg * 8:(hg + 1) * 8, :], writes=HS(j))
            P.act(lambda e, stg=stg, xv=xv: e.activation(out=xv, in_=stg, func=AF.Exp),
                  reads=HS(j) + [("h", j)], writes=[("xb", j)])
            if dl == 0:
                P.pool(lambda e, xv=xv: e.memset(xv[64:128, :, 0:64], 0.0), reads=[("xb", j)], writes=[("xb", j)])
            P.dma("pool", scrE[hg * 4:(hg + 1) * 4, :, dl, :, :].rearrange("hp p e q -> p hp e q"),
                  xb[j][:, 0:1024].rearrange("p (hp e q) -> p hp e q", hp=4, e=2), reads=[("xb", j)],
                  writes=[("scrE", hg * 4 + i4) for i4 in range(4)])

    cast_rr = {"i": 0}
    KTf = KT[:].rearrange("p a b c -> p (a b c)")
    stgF = [h[:, i, :] for i in range(NB)]
    stgF += [KTf[:, k * 4096:(k + 1) * 4096].bitcast(F32) for k in range(3)]
    for t_ in (hnT, ogcz, sg1):
        stgF.append(t_[:].rearrange("p a b -> p (a b)")[:, 0:4096].bitcast(F32))
    stgB = [xb[0][:], xb[1][:]]
    for t_ in (wbuf[0], wbuf[1]):
        tf = t_[:].rearrange("p a b -> p (a b)")
        stgB += [tf[:, k * 2048:(k + 1) * 2048] for k in range(4)]
    NSTG = min(len(stgF), len(stgB))
    STGKEYS = [("stgF", k, a) for k in range(NSTG) for a in range(4)] + [("stgB", k) for k in range(NSTG)]

    def convert_piece(src, out_view_of_B, n, store_dst, store_src_of_B, wkeys):
        k = nxt("x", NSTG)
        P.dma("sp", stgF[k][:, 0:n].rearrange("p (a w) -> p a w", a=4), src, writes=[("stgF", k, a) for a in range(4)])
        ce = cast_rr["i"] % 3
        cast_rr["i"] += 1
        eng = ("act", "dve", "pool")[ce]
        ov = out_view_of_B(stgB[k][:, 0:n])
        iv = stgF[k][:, 0:n] if len(ov.shape) == 2 else stgF[k][:, 0:n].rearrange("p (kc hp m) -> p kc hp m", kc=4, m=128)
        if ce == 0:
            fn = lambda e: e.copy(out=ov, in_=iv)
        else:
            fn = lambda e: e.tensor_copy(out=ov, in_=iv)
        P.op(eng, fn, reads=[("stgF", k, a) for a in range(4)], writes=[("stgB", k)])
        P.dma("pool", store_dst, store_src_of_B(stgB[k][:, 0:n]), reads=[("stgB", k)], writes=wkeys)

    def convert_sectioned(W, nsec, scr, kname):
        Wv = W.rearrange("(kc p) n -> p kc n", p=128)
        for j in range(nsec):
            for g in range(4):
                for kg in range(4):
                    src = Wv[:, kg * 4:(kg + 1) * 4, j * D + g * 512:j * D + (g + 1) * 512]
                    dst = scr[g * 4:(g + 1) * 4, :, j, kg * 4:(kg + 1) * 4, :].rearrange("hp p kc m -> p hp kc m")
                    convert_piece(src, lambda b: b.rearrange("p (hp kc m) -> p kc hp m", hp=4, m=128), 2048, dst,
                                  lambda b: b.rearrange("p (hp kc m) -> p hp kc m", hp=4, m=128),
                                  [(kname, g * 4 + i4) for i4 in range(4)])

    def convert_plain(W, scr, kname):
        Wv = W.rearrange("(kc p) n -> p kc n", p=128)
        for s_ in range(4):
            for kg in range(4):
                convert_piece(Wv[:, kg * 4:(kg + 1) * 4, s_ * 512:(s_ + 1) * 512], lambda b: b, 2048,
                              scr[s_, :, kg * 4:(kg + 1) * 4, :], lambda b: b.rearrange("p (a w) -> p a w", a=4),
                              [(kname, s_)])

    Wai = w_ai.rearrange("(kc p) n -> p kc n", p=128)
    Wci = w_ci.rearrange("(kc p) n -> p kc n", p=128)
    Wao = w_ao.rearrange("(kc p) n -> p kc n", p=128)
    Wco = w_co.rearrange("(kc p) n -> p kc n", p=128)
    for hp in range(NHP):
        for j in (1, 2):
            P.dma("pool", scrA[hp, :, j, :, :], Wai[:, :, j * D + hp * 128:j * D + (hp + 1) * 128],
                  writes=[("scrA", hp, j)])
    cvq = []
    for hp in range(NHP):
        for j in (0, 3):
            cvq.append((scrA[hp, :, j, :, :], Wai[:, :, j * D + hp * 128:j * D + (hp + 1) * 128], ("scrA", hp, j)))
    for s_ in range(4):
        for kg in range(4):
            cvq.append((scrO[s_, :, kg * 4:(kg + 1) * 4, :], Wao[:, kg * 4:(kg + 1) * 4, s_ * 512:(s_ + 1) * 512],
                        ("scrO", s_, kg)))
    for cc in range(16):
        for j in range(3):
            cvq.append((scrB[cc, :, j, :, :], Wci[:, :, j * D + cc * 128:j * D + (cc + 1) * 128], ("scrB", cc, j)))
    for s_ in range(4):
        for kg in range(4):
            cvq.append((scrP[s_, :, kg * 4:(kg + 1) * 4, :], Wco[:, kg * 4:(kg + 1) * 4, s_ * 512:(s_ + 1) * 512],
                        ("scrP", s_, kg)))

    def cv_pump(n, gate):
        for _ in range(n):
            if cvq:
                dst, src, wkey = cvq.pop(0)
                P.dma("pool", dst, src, reads=[gate], writes=[wkey])

    KA = lambda s: ("scrA", s)
    KO = lambda s: ("scrO", s)
    KB = lambda s: ("scrB", s)
    KP = lambda s: ("scrP", s)

    for s in range(4):
        P.dma("pool", kso[s], ck[s, 32:512, :])
        P.dma("pool", vso[s], cv[s, 32:512, :])

    slab_plan = []
    slab_state = {"cur": 0, "issued": 0}
    SCR = {"A": (scrA, KA), "O": (scrO, KO), "B": (scrB, KB), "P": (scrP, KP)}

    def issue_slab(idx):
        nm, s_, c0, c1 = slab_plan[idx]
        scr, key = SCR[nm]
        j = idx % NW
        wf = wbuf[j][:].rearrange("p a b -> p (a b)")
        if nm == "A":
            j0, j1 = (0, 4) if c0 == 0 else (1, 3)
            P.dma("sp", wf[:, j0 * 2048:j1 * 2048], scr[s_, :, j0:j1, :, :].rearrange("p j kc m -> p (j kc m)"),
                  reads=[("scrA", s_, jj) for jj in range(j0, j1)], writes=[("wbuf", j)])
        elif nm == "B":
            P.dma("sp", wf[:, 0:3 * 2048], scr[s_].rearrange("p j kc m -> p (j kc m)"),
                  reads=[("scrB", s_, jj) for jj in range(3)], writes=[("wbuf", j)])
        else:
            P.dma("sp", wbuf[j][:, :, c0:c1], scr[s_, :, :, c0:c1], reads=[(key(s_)[0], s_, kg) for kg in range(4)],
                  writes=[("wbuf", j)])

    def load_slab(scr, key, s, width=512, c0=0, c1=None):
        c1 = width if c1 is None else c1
        idx = slab_state["cur"]
        slab_state["cur"] = idx + 1
        nm = [k for k, v in SCR.items() if v[0] is scr][0]
        assert slab_plan[idx] == (nm, s, c0, c1), (slab_plan[idx], (nm, s, c0, c1))
        if nm == "A" and c0 == 0:
            P.dma("sp", Eb[s % 2][:], scrE[s], reads=[("scrE", s)], writes=[("Eb", s % 2)])
        while slab_state["issued"] <= min(idx + 1, len(slab_plan) - 1):
            issue_slab(slab_state["issued"])
            slab_state["issued"] += 1
        return idx % NW

    def w_unit(wj, col0, T, evac):
        pj = nxt("p", 2)
        wsec = wbuf[wj][:].rearrange("p a b -> p (a b)").rearrange("p (j kc m) -> p j kc m", j=4, kc=16)
        for kc in range(16):
            P.pe(lambda e, kc=kc: e.matmul(psP[pj][:, 0:T], lhsT=wsec[:, col0 // 128, kc, :],
                                           rhs=hnT[:, kc, 0:T], start=(kc == 0), stop=(kc == 15)),
                 reads=[("wbuf", wj)] + [("hnT", i) for i in range(NB)], writes=[("psP", pj)])
        evac(psP[pj], ("psP", pj))

    def rmsnorm_T(nb, grow):
        for i in range(nb):
            j = nxt("x", 2)
            P.act(lambda e, i=i, j=j: e.activation(out=xb[j][:], in_=h[:, i, :], func=AF.Square,
                                                   accum_out=ssq[:, i:i + 1]),
                  reads=[("h", i)], writes=[("xb", j), ("ssq", i)])
            P.dve(lambda e, i=i: e.tensor_scalar(out=rstd[:, i:i + 1], in0=ssq[:, i:i + 1], scalar1=1.0 / D, scalar2=RMS_EPS,
                                                 op0=ALU.mult, op1=ALU.add),
                  reads=[("ssq", i)], writes=[("rstd", i)])
            P.act(lambda e, i=i: e.activation(out=rstd[:, i:i + 1], in_=rstd[:, i:i + 1], func=AF.Sqrt),
                  reads=[("rstd", i)], writes=[("rstd", i)])
            P.dve(lambda e, i=i: e.reciprocal(out=rstd[:, i:i + 1], in_=rstd[:, i:i + 1]), reads=[("rstd", i)], writes=[("rstd", i)])
        for i in range(nb):
            j = nxt("x", 2)
            P.act(lambda e, i=i, j=j: e.activation(out=xb[j][:], in_=h[:, i, :], func=AF.Copy, scale=rstd[:, i:i + 1]),
                  reads=[("h", i), ("rstd", i)], writes=[("xb", j)])
            for g8 in range(2):
                tb_ = nxt("t", 2)
                for c in range(8):
                    kc = g8 * 8 + c
                    P.pe(lambda e, kc=kc, c=c, j=j, tb_=tb_: e.transpose(
                        psT[tb_][:, c * 128:(c + 1) * 128], xb[j][:, kc * 128:(kc + 1) * 128], identb[:]),
                        reads=[("xb", j), "identb"], writes=[("psT", tb_)])
                P.dve(lambda e, g8=g8, i=i, tb_=tb_: e.tensor_tensor(
                    out=hnT[:, g8 * 8:(g8 + 1) * 8, i * 128:(i + 1) * 128],
                    in0=psT[tb_][:].rearrange("p (c t) -> p c t", t=128),
                    in1=prmT[:, g8 * 8:(g8 + 1) * 8, grow:grow + 1].to_broadcast([128, 8, 128]), op=ALU.mult),
                    reads=[("psT", tb_), "prmT"], writes=[("hnT", i)])

    def out_proj(nb, scr, key, src):
        for ng in range(4):
            wj = load_slab(scr, key, ng)
            for i in range(nb):
                pj = nxt("p", 2)
                for kc in range(16):
                    P.pe(lambda e, kc=kc, i=i, pj=pj, wj=wj: e.matmul(
                        psP[pj][:], lhsT=src[:, kc, i * 128:(i + 1) * 128], rhs=wbuf[wj][:, kc, :],
                        start=(kc == 0), stop=(kc == 15)),
                        reads=[("wbuf", wj), ("ogcz", kc)], writes=[("psP", pj)])
                P.dve(lambda e, i=i, pj=pj, ng=ng: e.tensor_tensor(
                    out=h[:, i, ng * 512:(ng + 1) * 512], in0=h[:, i, ng * 512:(ng + 1) * 512], in1=psP[pj][:], op=ALU.add),
                    reads=[("h", i), ("psP", pj)], writes=[("h", i)])

    def final_norm(nb, dsts):
        for i in range(nb):
            j = nxt("x", 2)
            P.act(lambda e, i=i, j=j: e.activation(out=xb[j][:], in_=h[:, i, :], func=AF.Square,
                                                   accum_out=ssq[:, i:i + 1]),
                  reads=[("h", i)], writes=[("xb", j), ("ssq", i)])
        P.dve(lambda e: e.tensor_scalar(out=rstd[:, 0:nb], in0=ssq[:, 0:nb], scalar1=1.0 / D, scalar2=RMS_EPS,
                                        op0=ALU.mult, op1=ALU.add),
              reads=[("ssq", i) for i in range(nb)], writes=[("rstd", i) for i in range(nb)])
        P.act(lambda e: e.activation(out=rstd[:, 0:nb], in_=rstd[:, 0:nb], func=AF.Sqrt), reads=[("rstd", i) for i in range(nb)], writes=[("rstd", i) for i in range(nb)])
        P.dve(lambda e: e.reciprocal(out=rstd[:, 0:nb], in_=rstd[:, 0:nb]), reads=[("rstd", i) for i in range(nb)], writes=[("rstd", i) for i in range(nb)])
        SGK = [("sg1", cc) for cc in range(16)]
        fgv = sg1[:].rearrange("p a b -> p (a b)")[:, 0:2 * D].bitcast(F32)
        P.dma("sp", fgv, fgd, writes=SGK)
        for i in range(nb):
            if dsts[i] is None:
                continue
            P.dve(lambda e, i=i: e.scalar_tensor_tensor(out=h[:, i, :], in0=h[:, i, :], scalar=rstd[:, i:i + 1],
                                                        in1=fgv, op0=ALU.mult, op1=ALU.mult),
                  reads=[("h", i), ("rstd", i)] + SGK, writes=[("h", i)])
            P.dma("pool", dsts[i], h[:, i, :], reads=[("h", i)])

    def kv_out_unit(wj, hp, i, dst, r0):
        pj = nxt("p", 2)
        wsec = wbuf[wj][:].rearrange("p a b -> p (a b)").rearrange("p (j kc m) -> p j kc m", j=4, kc=16)
        for kc in range(16):
            P.pe(lambda e, kc=kc: e.matmul(psP[pj][:, 0:256].rearrange("p (a b) -> p a b", b=128),
                                           lhsT=hnT[:, kc, i * 128:(i + 1) * 128],
                                           rhs=wsec[:, 1:3, kc, :], start=(kc == 0), stop=(kc == 15)),
                 reads=[("wbuf", wj)] + [("hnT", ii) for ii in range(NB)], writes=[("psP", pj)])
        kj = 0
        P.act(lambda e: e.copy(out=kvo[kj][:], in_=psP[pj][:, 0:256].rearrange("p (a b) -> p a b", b=128)),
              reads=[("psP", pj)], writes=[("kvo", kj)])
        P.dma("pool", dst[:, r0:r0 + 128, hp * 128:(hp + 1) * 128].rearrange("a r c -> r a c"), kvo[kj][:],
              reads=[("kvo", kj)])

    def l0_proj_fillers(hp, T, blocks, smp, par, kind, wj, kvout):
        fl = []

        def fq():
            w_unit(wj, 0, T, lambda ps, k: P.act(
                lambda e: e.copy(out=QT[par][:, 0:T], in_=ps[:, 0:T]), reads=[k], writes=[("QT", par)]))

        def fk():
            def ev(ps, k):
                if smp:
                    so_ = len(blocks) * 128
                    P.act(lambda e: e.copy(out=KTn[par][:], in_=ps[:, so_:so_ + 128]), reads=[k], writes=[("KTn", par)])
                for i in range(len(blocks)):
                    sl = blocks[i] % NSLOT
                    P.dve(lambda e, i=i, sl=sl: e.tensor_copy(out=KT[:, hp, sl, :], in_=ps[:, i * 128:(i + 1) * 128]),
                          reads=[k], writes=[("KT", hp, sl)])
            w_unit(wj, 128, T, ev)

        def fv():
            w_unit(wj, 256, T, lambda ps, k: P.act(
                lambda e: e.copy(out=VT[par][:, 0:T], in_=ps[:, 0:T]), reads=[k], writes=[("VT", par)]))

        def fv2():
            if not blocks:
                return
            r = nxt("t", 2)
            for i in range(len(blocks)):
                P.pe(lambda e, i=i: e.transpose(psT[r][:, i * 128:(i + 1) * 128], VT[par][:, i * 128:(i + 1) * 128], identb[:]),
                     reads=[("VT", par), "identb"], writes=[("psT", r)])
            for i in range(len(blocks)):
                sl = blocks[i] % NSLOT
                P.dve(lambda e, sl=sl, i=i: e.tensor_copy(
                    out=V[:, sl, 2 * hp:2 * hp + 2, 0:HD],
                    in_=psT[r][:, i * 128:(i + 1) * 128].rearrange("p (a b) -> p a b", b=HD)),
                    reads=[("psT", r)], writes=[("V", sl, hp)])

        def fg_():
            def ev(ps, k):
                P.act(lambda e: e.activation(out=tg[par][:, 0:T], in_=ps[:, 0:T], func=AF.Tanh, scale=0.5),
                      reads=[k], writes=[("ah", par)])
                P.dve(lambda e: e.scalar_tensor_tensor(out=sgT[par][:, 0:T], in0=tg[par][:, 0:T], scalar=1.0,
                                                       in1=ps[:, 0:T], op0=ALU.add, op1=ALU.mult),
                      reads=[k, ("ah", par)], writes=[("sgT", par)])
            w_unit(wj, 384, T, ev)

        if kind == "kv":
            fl += [fv, fk, fv2]
        else:
            fl += [fv, fq, fk, fv2, fg_]
        for (i, dst, r0) in kvout:
            fl.append(lambda i=i, dst=dst, r0=r0: kv_out_unit(wj, hp, i, dst, r0))
        return fl

    def att_qk(hp, par, i, b, e_, u):
        hq = slice(64 * e_, 64 * e_ + 64)
        s1 = nxt("s", 3)
        s2 = nxt("s", 3)
        for dl in range(5):
            sl = (b - dl) % NSLOT
            bank = s1 if dl < 2 else s2
            col = dl * 128 if dl < 2 else (dl - 2) * 128
            P.pe(lambda e, sl=sl, bank=bank, col=col: e.matmul(
                psS[bank][:, col:col + 128], lhsT=KT[hq, hp, sl, :], rhs=QT[par][hq, i * 128:(i + 1) * 128],
                start=True, stop=True),
                reads=[("KT", hp, sl), ("QT", par)], writes=[("psS", bank)])
        return (s1, s2)

    def att_exp(hp, e_, u, banks):
        hh = 2 * hp + e_
        s1, s2 = banks
        b1 = ("psS", s2)
        b0 = ("psS", s1)
        pk = ("P", u)
        P.act(lambda e: e.activation(out=et[u][:], in_=psS[s1][:, 0:256].rearrange("p (a b) -> p a b", b=128),
                                     func=AF.Exp, scale=0.125),
              reads=[b0], writes=[("et", u)])
        P.dve(lambda e: e.tensor_tensor(out=Pb[u][:, 0:2, :], in0=et[u][:], in1=Eb[hp % 2][:, :, e_, :], op=ALU.mult),
              reads=[("et", u), ("Eb", hp % 2)], writes=[pk])
        P.act(lambda e: e.activation(out=Pb[u][:, 2:4, :], in_=psS[s2][:, 0:256].rearrange("p (a b) -> p a b", b=128),
                                     func=AF.Exp, bias=cvec[:, hh:hh + 1], scale=0.125),
              reads=[b1, "cvec"], writes=[pk])
        P.act(lambda e: e.activation(out=Pb[u][:, 4, 0:64], in_=psS[s2][:, 256:320],
                                     func=AF.Exp, bias=cvec[:, hh:hh + 1], scale=0.125),
              reads=[b1, "cvec"], writes=[pk])
        P.act(lambda e: e.activation(out=Pb[u][64:128, 4, 64:128], in_=psS[s2][64:128, 320:384],
                                     func=AF.Exp, bias=cvec[64:128, hh:hh + 1], scale=0.125),
              reads=[b1, "cvec"], writes=[pk])

    def att_pv(hp, i, b, e_, u):
        hh = 2 * hp + e_
        for dl in range(5):
            sl = (b - dl) % NSLOT
            P.pe(lambda e, dl=dl, sl=sl: e.matmul(
                psO[:, (i * 2 + e_) * 65:(i * 2 + e_ + 1) * 65], lhsT=Pb[u][:, dl, :], rhs=V[:, sl, hh, :],
                start=(dl == 0), stop=(dl == 4)),
                reads=[("P", u), ("V", sl, hp), ("Vone", sl)], writes=[("psO",)])

    def att_finish(hp, par, i, q0, nq):
        rj = nxt("u", 2)
        ov = psO[0:nq, i * 130:(i + 1) * 130].rearrange("p (a b) -> p a b", b=65)
        P.dve(lambda e: e.tensor_scalar(out=rec[rj][0:nq], in0=ov[:, :, 64:65], scalar1=1e-30, scalar2=None, op0=ALU.add),
              reads=[("psO",)], writes=[("rec", rj)])
        P.dve(lambda e: e.reciprocal(out=rec[rj][0:nq], in_=rec[rj][0:nq]), reads=[("rec", rj)], writes=[("rec", rj)])
        P.dve(lambda e: e.tensor_tensor(out=on[rj][0:nq], in0=ov[:, :, 0:64], in1=rec[rj][0:nq].to_broadcast([nq, 2, HD]),
                                        op=ALU.mult),
              reads=[("psO",), ("rec", rj)], writes=[("on", rj)])

        def part2():
            r = nxt("t", 2)
            P.pe(lambda e: e.transpose(psT[r][:, 0:nq], on[rj][0:nq].rearrange("p a b -> p (a b)"), identb[0:nq, 0:nq]),
                 reads=[("on", rj), "identb"], writes=[("psT", r)])
            P.dve(lambda e: e.tensor_tensor(out=ogcz[:, hp, q0:q0 + nq], in0=psT[r][:, 0:nq],
                                            in1=sgT[par][:, q0:q0 + nq], op=ALU.mult),
                  reads=[("psT", r), ("sgT", par)], writes=[("ogcz", hp)])
        return part2

    def sample_attention(hp, par, fillers, so, pre_done=False, only_pre=False):
        units = [(s, e_) for s in range(4) for e_ in range(2)]
        banks = {}

        def load_a(s):
            j = s % 2
            P.dma("sp", kcf[j][:], ck[s, :, hp * 128:(hp + 1) * 128].rearrange("(a p) c -> p a c", p=128),
                  writes=[("kcf", 0)])
            P.dve(lambda e: e.tensor_copy(out=kcb[j][:], in_=kcf[j][:]), reads=[("kcf", 0)], writes=[("kcb", j)])
            P.dma("sp", vcf[j][:], cv[s, :, hp * 128:(hp + 1) * 128].rearrange("(a p) c -> p a c", p=128),
                  writes=[("vcf", 0)])
            P.pool(lambda e: e.tensor_copy(out=Vc[j][:, :, :, 0:HD], in_=vcf[j][:].rearrange("p a (t d) -> p a t d", d=HD)),
                   reads=[("vcf", 0)], writes=[("Vc", j)])

        if only_pre:
            load_a(0)
            return None

        def load_b(s):
            j = s % 2
            tb_ = nxt("t", 2)
            for a in range(4):
                P.pe(lambda e, a=a: e.transpose(psT[tb_][:, a * 128:(a + 1) * 128], kcb[j][:, a, :], identb[:]),
                     reads=[("kcb", j), "identb"], writes=[("psT", tb_)])
            P.act(lambda e: e.copy(out=KTc[j][:], in_=psT[tb_][:, 0:512]), reads=[("psT", tb_)], writes=[("KTc", j)])
            r = nxt("t", 2)
            P.pe(lambda e: e.transpose(psT[r][0:32, 0:128], VT[par][:, so + s * 32:so + (s + 1) * 32], identb[:]),
                 reads=[("VT", par), "identb"], writes=[("psT", r)])
            P.dve(lambda e: e.tensor_copy(out=Vn[j][:, :, 0:HD], in_=psT[r][0:32, 0:128].rearrange("p (a b) -> p a b", b=HD)),
                  reads=[("psT", r)], writes=[("Vn", j)])

        def qk(ui):
            s, e_ = units[ui]
            j = s % 2
            hq = slice(64 * e_, 64 * e_ + 64)
            q = QT[par][hq, so + s * 32:so + (s + 1) * 32]
            s1 = nxt("s", 3)
            s2 = nxt("s", 3)
            banks[ui] = (s1, s2)
            for a in range(3):
                P.pe(lambda e, a=a: e.matmul(psS[s2][:, a * 32:(a + 1) * 32], lhsT=KTc[j][hq, a * 128:(a + 1) * 128],
                                             rhs=q, start=True, stop=True),
                     reads=[("KTc", j), ("QT", par)], writes=[("psS", s2)])
            P.pe(lambda e: e.matmul(psS[s1][:, 0:32], lhsT=KTc[j][hq, 384:512], rhs=q, start=True, stop=True),
                 reads=[("KTc", j), ("QT", par)], writes=[("psS", s1)])
            P.pe(lambda e: e.matmul(psS[s1][0:32, 32:64], lhsT=KTn[par][hq, s * 32:(s + 1) * 32], rhs=q,
                                    start=True, stop=True),
                 reads=[("KTn", par), ("QT", par)], writes=[("psS", s1)])

        def ex(ui):
            s, e_ = units[ui]
            u = ui % 2
            hh = 2 * hp + e_
            s1, s2 = banks[ui]
            pk = ("P", u)
            P.act(lambda e: e.activation(out=Pb[u][:, 2, 0:96], in_=psS[s2][:, 0:96], func=AF.Exp,
                                         bias=cvec[:, hh:hh + 1], scale=0.125),
                  reads=[("psS", s2), "cvec"], writes=[pk])
            P.act(lambda e: e.activation(out=et[u][:, 0, 0:32], in_=psS[s1][:, 0:32], func=AF.Exp, scale=0.125),
                  reads=[("psS", s1)], writes=[("et", u)])
            P.act(lambda e: e.activation(out=et[u][0:32, 1, 0:32], in_=psS[s1][0:32, 32:64], func=AF.Exp, scale=0.125),
                  reads=[("psS", s1)], writes=[("et", u)])
            P.dve(lambda e: e.tensor_tensor(out=Pb[u][:, 0, 0:32], in0=et[u][:, 0, 0:32], in1=Eb[hp % 2][:, 1, e_, 0:32], op=ALU.mult),
                  reads=[("et", u), ("Eb", hp % 2)], writes=[pk])
            P.dve(lambda e: e.tensor_tensor(out=Pb[u][0:32, 1, 0:32], in0=et[u][0:32, 1, 0:32], in1=Eb[hp % 2][0:32, 0, e_, 0:32], op=ALU.mult),
                  reads=[("et", u), ("Eb", hp % 2)], writes=[pk])

        def pv(ui):
            s, e_ = units[ui]
            u = ui % 2
            j = s % 2
            sr = s % 2
            oreg = psO[0:32, (sr * 2 + e_) * 65:(sr * 2 + e_ + 1) * 65]
            for a in range(3):
                P.pe(lambda e, a=a: e.matmul(oreg, lhsT=Pb[u][:, 2, a * 32:(a + 1) * 32], rhs=Vc[j][:, a, e_, :],
                                             start=(a == 0), stop=False),
                     reads=[("P", u), ("Vc", j)], writes=[("psO",)])
            P.pe(lambda e: e.matmul(oreg, lhsT=Pb[u][:, 0, 0:32], rhs=Vc[j][:, 3, e_, :], start=False, stop=False),
                 reads=[("P", u), ("Vc", j)], writes=[("psO",)])
            P.pe(lambda e: e.matmul(oreg, lhsT=Pb[u][0:32, 1, 0:32], rhs=Vn[j][:, e_, :], start=False, stop=True),
                 reads=[("P", u), ("Vn", j)], writes=[("psO",)])

        if not pre_done:
            load_a(0)
        load_b(0)
        qk(0); ex(0)
        dfr = []
        for ui in range(len(units)):
            s, e_ = units[ui]
            if e_ == 0 and s + 1 < 4:
                load_a(s + 1)
            if e_ == 1 and s + 1 < 4:
                load_b(s + 1)
            if ui + 1 < len(units):
                qk(ui + 1); ex(ui + 1)
            if fillers:
                fillers.pop(0)()
            pv(ui)
            if dfr and e_ == 1:
                dfr.pop(0)()
            if e_ == 1:
                dfr.append(att_finish_sample(hp, par, s, so))
        while fillers:
            fillers.pop(0)()
        while dfr:
            dfr.pop(0)()
        return load_a

    def att_finish_sample(hp, par, s, so):
        rj = nxt("u", 2)
        sr = s % 2
        ov = psO[0:32, sr * 130:(sr + 1) * 130].rearrange("p (a b) -> p a b", b=65)
        P.dve(lambda e: e.tensor_scalar(out=rec[rj][0:32], in0=ov[:, :, 64:65], scalar1=1e-30, scalar2=None, op0=ALU.add),
              reads=[("psO",)], writes=[("rec", rj)])
        P.dve(lambda e: e.reciprocal(out=rec[rj][0:32], in_=rec[rj][0:32]), reads=[("rec", rj)], writes=[("rec", rj)])
        P.dve(lambda e: e.tensor_tensor(out=on[rj][0:32], in0=ov[:, :, 0:64], in1=rec[rj][0:32].to_broadcast([32, 2, HD]),
                                        op=ALU.mult),
              reads=[("psO",), ("rec", rj)], writes=[("on", rj)])
        def part2():
            r = nxt("t", 2)
            P.pe(lambda e: e.transpose(psT[r][:, 0:32], on[rj][0:32].rearrange("p a b -> p (a b)"), identb[0:32, 0:32]),
                 reads=[("on", rj), "identb"], writes=[("psT", r)])
            P.dve(lambda e: e.tensor_tensor(out=ogcz[:, hp, so + s * 32:so + (s + 1) * 32], in0=psT[r][:, 0:32],
                                            in1=sgT[par][:, so + s * 32:so + (s + 1) * 32], op=ALU.mult),
                  reads=[("psT", r), ("sgT", par)], writes=[("ogcz", hp)])
        return part2

    def layer0(kind, blocks, smp, kvout_blocks):
        npb = len(blocks)
        nb = npb + (1 if smp else 0)
        T = nb * 128
        so = npb * 128
        if kind == "kv":
            for hp in range(NHP):
                wj = load_slab(scrA, KA, hp, c0=128, c1=384)
                for f in l0_proj_fillers(hp, T, blocks, False, hp % 2, kind, wj, []):
                    f()
                cv_pump(2 if blocks[0] == 0 else 1, ("VT", hp % 2))
            return
        deferred = []
        for j in range(2):
            P.pool(lambda e, j=j: e.memset(Pb[j][0:64, 4, 64:128], 0.0), writes=[("P", j)])
        wjs = {0: load_slab(scrA, KA, 0)}
        for f in l0_proj_fillers(0, T, blocks, smp, 0, kind, wjs[0], kvout_blocks):
            f()
        for hp in range(NHP):
            par = hp % 2
            fillers = []
            if hp + 1 < NHP:
                wjs[hp + 1] = load_slab(scrA, KA, hp + 1)
                fillers = l0_proj_fillers(hp + 1, T, blocks, smp, (hp + 1) % 2, kind, wjs[hp + 1], kvout_blocks)
            units = [(i, e_) for i in range(npb) for e_ in range(2)]
            if smp:
                sample_attention(hp, par, None, so, only_pre=True)
            if units:
                i0, e0 = units[0]
                bk = att_qk(hp, par, i0, blocks[i0], e0, 0)
                att_exp(hp, e0, 0, bk)
            for ui in range(len(units)):
                i, e_ = units[ui]
                if ui + 1 < len(units):
                    i2, e2 = units[ui + 1]
                    bk = att_qk(hp, par, i2, blocks[i2], e2, (ui + 1) % 2)
                    att_exp(hp, e2, (ui + 1) % 2, bk)
                if fillers:
                    fillers.pop(0)()
                att_pv(hp, i, blocks[i], e_, ui % 2)
                if deferred and e_ == 1:
                    deferred.pop(0)()
                if e_ == 1:
                    deferred.append(att_finish(hp, par, i, i * 128, 128))
            if smp:
                while deferred:
                    deferred.pop(0)()
                sample_attention(hp, par, fillers, so, pre_done=True)
            while fillers:
                fillers.pop(0)()
            if len(deferred) > 1:
                deferred.pop(0)()
            cv_pump(4, ("VT", (hp + 1) % 2))
        while deferred:
            deferred.pop(0)()
        cv_pump(len(cvq), ("VT", 0))

    def layer1(npb, smp, mask_first):
        nb = npb + (1 if smp else 0)
        T = nb * 128
        Tp = npb * 128
        pw = (CK + Tp) if npb else 0
        sw = CK + 32
        used = pw + (4 * sw if smp else 0)
        pend = []
        ccvS = KTf[:, 0:4096].bitcast(F32)
        CCK = [("KT", hp, sl) for hp in range(5) for sl in range(NSLOT)]

        def stats(cc):
            P.pe(lambda e: e.matmul(psS[0][:, 0:T], lhsT=ones_bf[:], rhs=ogcz[:, cc, 0:T], start=(cc == 0), stop=(cc == 15)),
                 reads=[("ogcz", cc), "ones"], writes=[("psS", 0)])
            jc = cc % 2
            P.pe(lambda e: e.matmul(psS[1][:, 0:T], lhsT=ones_bf[:], rhs=c2[jc][:, 0:T], start=(cc == 0), stop=(cc == 15)),
                 reads=[("c2", jc), "ones"], writes=[("psS", 1)])

        convq = []

        def conv_unit(cc, j):
            pj = nxt("p", 2)
            if npb:
                for k in range(CW):
                    P.pe(lambda e, k=k: e.matmul(psP[pj][:, 0:Tp], lhsT=dgb[0][:, k, :], rhs=ubb[j][:, k:k + Tp],
                                                 start=(k == 0), stop=(k == CW - 1)),
                         reads=[("dgb", 0), ("ubb", j)] + L0KEYS, writes=[("psP", pj)])
            if smp:
                ubs_b = ubb[j][:, pw:pw + 4 * sw].rearrange("p (s w) -> p s w", w=sw)
                outv = psP[pj][:, Tp:T].rearrange("p (s w) -> p s w", w=32)
                for k in range(CW):
                    P.pe(lambda e, k=k: e.matmul(outv, lhsT=dgb[0][:, k, :], rhs=ubs_b[:, :, k:k + 32],
                                                 start=(k == 0), stop=(k == CW - 1)),
                         reads=[("dgb", 0), ("ubb", j)] + L0KEYS, writes=[("psP", pj)])
            P.act(lambda e: e.activation(out=ogcz[:, cc, 0:T], in_=psP[pj][:, 0:T], func=AF.Identity,
                                         bias=prmT[:, cc, 31:32], scale=1.0),
                  reads=[("psP", pj), "prmT"], writes=[("ogcz", cc)])
            P.act(lambda e: e.activation(out=c2[j][:, 0:T], in_=psP[pj][:, 0:T], func=AF.Square,
                                         bias=prmT[:, cc, 31:32], scale=1.0),
                  reads=[("psP", pj), "prmT"], writes=[("c2", j)])
            pend.append(cc)
            if len(pend) > 1:
                stats(pend.pop(0))

        if smp:
            P.dma("sp", ccvS[0:4 * CK, :], ccv, writes=CCK)
        for cc in range(16):
            wj = load_slab(scrB, KB, cc, width=384)
            j = cc % 2
            ubs = ub[j][:, pw:pw + 4 * sw].rearrange("p (s w) -> p s w", w=sw) if smp else None
            if smp:
                P.pe(lambda e, cc=cc: e.matmul(psS[2][:, 0:4 * CK], lhsT=ccvS[0:4 * CK, cc * 128:(cc + 1) * 128],
                                               rhs=identf[0:4 * CK, 0:4 * CK], start=True, stop=True),
                     reads=CCK + ["identf"], writes=[("psS", 2)])
                P.act(lambda e, ubs=ubs: e.copy(out=ubs[:, :, 0:CK], in_=psS[2][:, 0:4 * CK].rearrange("p (s w) -> p s w", w=CK)),
                      reads=[("psS", 2)], writes=[("ub", 0)])
            if npb:
                P.act(lambda e, cc=cc, j=j: e.copy(out=ub[j][:, 0:CK], in_=ust[:, cc, :]),
                      reads=[("ust", cc)], writes=[("ub", 0)])
            def ev_a(ps, k, cc=cc, j=j):
                P.act(lambda e: e.activation(out=ah[j][:, 0:T], in_=ps[:, 0:T], func=AF.Identity,
                                             bias=prmH[:, cc, 2:3], scale=0.5),
                      reads=[k, "prmH"], writes=[("ah", j)])
            w_unit(wj, 0, T, ev_a)

            def ev_b(ps, k, cc=cc, j=j, ubs=ubs):
                P.act(lambda e: e.activation(out=tb[j][:, 0:T], in_=ps[:, 0:T], func=AF.Tanh,
                                             bias=prmH[:, cc, 3:4], scale=0.5),
                      reads=[k, "prmH"], writes=[("tb", j)])
                if npb:
                    P.dve(lambda e: e.scalar_tensor_tensor(
                        out=ub[j][:, CK:CK + Tp], in0=tb[j][:, 0:Tp], scalar=1.0, in1=ah[j][:, 0:Tp],
                        op0=ALU.add, op1=ALU.mult),
                        reads=[("tb", j), ("ah", j)], writes=[("ub", 0)])
                if smp:
                    P.dve(lambda e: e.scalar_tensor_tensor(
                        out=ubs[:, :, CK:CK + 32], in0=tb[j][:, Tp:T].rearrange("p (s w) -> p s w", w=32), scalar=1.0,
                        in1=ah[j][:, Tp:T].rearrange("p (s w) -> p s w", w=32), op0=ALU.add, op1=ALU.mult),
                        reads=[("tb", j), ("ah", j)], writes=[("ub", 0)])
                if mask_first:
                    P.dve(lambda e: e.tensor_scalar(out=ub[j][:, CK:CK + 128], in0=ub[j][:, CK:CK + 128],
                                                    scalar1=hmask[:, 0:1], scalar2=None, op0=ALU.mult),
                          reads=[("ub", 0), "hmask"], writes=[("ub", 0)])
            w_unit(wj, 128, T, ev_b)

            def ev_g(ps, k, cc=cc, j=j):
                P.act(lambda e: e.activation(out=vh[j][:, 0:T], in_=ps[:, 0:T], func=AF.Identity,
                                             bias=prmH[:, cc, 4:5], scale=0.5),
                      reads=[k, "prmH"], writes=[("vh", j)])
                P.act(lambda e: e.activation(out=tt[j][:, 0:T], in_=ps[:, 0:T], func=AF.Tanh,
                                             bias=prmH[:, cc, 4:5], scale=0.5),
                      reads=[k, "prmH"], writes=[("tt", j)])
                P.pool(lambda e: e.tensor_tensor(out=tt[j][:, 0:T], in0=tt[j][:, 0:T], in1=vh[j][:, 0:T], op=ALU.mult),
                       reads=[("tt", j), ("vh", j)], writes=[("tt", j)])
                P.pool(lambda e: e.tensor_tensor(out=sg1[:, cc, 0:T], in0=tt[j][:, 0:T], in1=vh[j][:, 0:T], op=ALU.add),
                       reads=[("tt", j), ("vh", j)], writes=[("sg1", cc)])
            w_unit(wj, 256, T, ev_g)
            if convq:
                conv_unit(*convq.pop(0))
            P.act(lambda e, j=j: e.copy(out=ubb[j][:, 0:used], in_=ub[j][:, 0:used]),
                  reads=[("ub", 0)], writes=[("ubb", j)])
            P.dve(lambda e, cc=cc: e.tensor_tensor(
                out=dgb[0][:], in0=identb[:].unsqueeze(1).to_broadcast([128, CW, 128]),
                in1=prmT[:, cc, 0:CW].unsqueeze(2).to_broadcast([128, CW, 128]), op=ALU.mult),
                reads=["identb", "prmT"], writes=[("dgb", 0)] + L0KEYS)
            convq.append((cc, j))
            if smp:
                for s_ in range(4):
                    P.pe(lambda e, s_=s_, ubs=ubs: e.matmul(psS[2][0:CK, s_ * 128:(s_ + 1) * 128], lhsT=ubs[:, s_, 32:32 + CK],
                                                            rhs=identf[:], start=True, stop=True),
                         reads=[("ub", 0), "identf"], writes=[("psS", 2)])
                P.act(lambda e: e.copy(out=cst[0:CK], in_=psS[2][0:CK, 0:512].rearrange("p (s c) -> p s c", c=128)),
                      reads=[("psS", 2)], writes=["cst"])
                P.dma("pool", csd[:, :, cc * 128:(cc + 1) * 128].rearrange("s r c -> r s c"), cst[0:CK], reads=["cst"])
            if npb:
                P.act(lambda e, cc=cc, j=j: e.copy(out=ust[:, cc, :], in_=ub[j][:, Tp:Tp + CK]),
                      reads=[("ub", 0)], writes=[("ust", cc)])
        while convq:
            conv_unit(*convq.pop(0))
        while pend:
            stats(pend.pop(0))
        P.dve(lambda e, j=j: e.tensor_copy(out=mu[:, 0:T], in_=psS[0][:, 0:T]), reads=[("psS", 0)], writes=["mu"])
        P.dve(lambda e, j=j: e.tensor_tensor(out=rsd[:, 0:T], in0=mu[:, 0:T], in1=mu[:, 0:T], op=ALU.mult), reads=["mu"], writes=["rsd"])
        P.dve(lambda e, j=j: e.tensor_tensor(out=rsd[:, 0:T], in0=psS[1][:, 0:T], in1=rsd[:, 0:T], op=ALU.subtract),
              reads=[("psS", 1), "rsd"], writes=["rsd"])
        P.dve(lambda e, j=j: e.tensor_scalar(out=rsd[:, 0:T], in0=rsd[:, 0:T], scalar1=LN_EPS, scalar2=None, op0=ALU.add),
              reads=["rsd"], writes=["rsd"])
        P.act(lambda e, j=j: e.activation(out=rsd[:, 0:T], in_=rsd[:, 0:T], func=AF.Sqrt), reads=["rsd"], writes=["rsd"])
        P.dve(lambda e, j=j: e.reciprocal(out=rsd[:, 0:T], in_=rsd[:, 0:T]), reads=["rsd"], writes=["rsd"])
        T1 = [(ah[0], ("ah", 0)), (ah[1], ("ah", 1))]
        TT = [(tt[0], ("tt", 0)), (tt[1], ("tt", 1))]
        VH = [(vh[0], ("vh", 0)), (vh[1], ("vh", 1))]
        def z_front(cc):
            t1b, t1k = T1[cc % 2]
            ttb, ttk = TT[cc % 2]
            vhb, vhk = VH[cc % 2]
            P.dve(lambda e: e.tensor_tensor(out=t1b[:, 0:T], in0=ogcz[:, cc, 0:T], in1=mu[:, 0:T], op=ALU.subtract),
                  reads=[("ogcz", cc), "mu"], writes=[t1k])
            P.dve(lambda e: e.tensor_tensor(out=t1b[:, 0:T], in0=t1b[:, 0:T], in1=rsd[:, 0:T], op=ALU.mult),
                  reads=[t1k, "rsd"], writes=[t1k])
            P.act(lambda e: e.activation(out=ttb[:, 0:T], in_=t1b[:, 0:T], func=AF.Tanh,
                                         bias=prmH[:, cc, 1:2], scale=prmH[:, cc, 0:1]),
                  reads=[t1k, "prmH"], writes=[ttk])
            P.act(lambda e: e.activation(out=vhb[:, 0:T], in_=t1b[:, 0:T], func=AF.Identity,
                                         bias=prmH[:, cc, 1:2], scale=prmH[:, cc, 0:1]),
                  reads=[t1k, "prmH"], writes=[vhk])

        def z_back(cc):
            ttb, ttk = TT[cc % 2]
            vhb, vhk = VH[cc % 2]
            P.dve(lambda e: e.scalar_tensor_tensor(out=vhb[:, 0:T], in0=ttb[:, 0:T], scalar=1.0, in1=vhb[:, 0:T],
                                                   op0=ALU.add, op1=ALU.mult),
                  reads=[ttk, vhk], writes=[vhk])
            P.pool(lambda e: e.tensor_tensor(out=ogcz[:, cc, 0:T], in0=vhb[:, 0:T], in1=sg1[:, cc, 0:T], op=ALU.mult),
                   reads=[vhk, ("sg1", cc)], writes=[("ogcz", cc)])

        z_front(0)
        for cc in range(16):
            if cc + 1 < 16:
                z_front(cc + 1)
            z_back(cc)

    ones_bf = P.sb("ones_bf", [128, 128], BF16)
    P.pool(lambda e: e.memset(ones_bf[:], 1.0 / D), writes=["ones"])

    tiles = []
    for b in range(0, 4, NB):
        tiles.append(("kv", list(range(b, min(b + NB, 4))), False))
    full = [list(range(b, min(b + NB, NPB))) for b in range(4, NPB, NB)]
    if max_full_tiles is not None:
        full = full[:max_full_tiles]
    if len(full) and len(full[-1]) < NB and max_full_tiles is None:
        for bl in full[:-1]:
            tiles.append(("full", bl, False))
        tiles.append(("full", full[-1], True))
    else:
        for bl in full:
            tiles.append(("full", bl, False))
        tiles.append(("full", [], True))

    for (kind, blocks, smp) in tiles:
        if kind == "kv":
            slab_plan.extend([("A", hp, 128, 384) for hp in range(NHP)])
        else:
            slab_plan.extend([("A", hp, 0, 512) for hp in range(NHP)] + [("O", g, 0, 512) for g in range(4)]
                             + [("B", cc, 0, 384) for cc in range(16)] + [("P", g, 0, 512) for g in range(4)])

    for (kind, blocks, smp) in tiles:
        npb = len(blocks)
        nb = npb + (1 if smp else 0)
        T = nb * 128
        for i, b in enumerate(blocks):
            P.dma("sp", h[:, i, :], xp[b * 128:(b + 1) * 128, :], writes=[("h", i)])
        if smp:
            P.dma("sp", h[:, npb, :], xs, writes=[("h", npb)])
        for b in blocks:
            sl = b % NSLOT
            keys = [("V", sl, hp) for hp in range(NHP)] + [("Vone", sl)]
            if b < HALO_BLKS:
                P.pool(lambda e, sl=sl: e.tensor_scalar(out=V[:, sl, :, HD:HD + 1], in0=twos[:], scalar1=hmask[:, 0:1],
                                                        scalar2=None, op0=ALU.mult),
                       reads=["twos", "hmask"], writes=keys)
            else:
                P.pool(lambda e, sl=sl: e.tensor_copy(out=V[:, sl, :, HD:HD + 1], in_=twos[:]), reads=["twos"], writes=keys)
        rmsnorm_T(nb, 37)
        kvout = []
        if kind == "full":
            for i, b in enumerate(blocks):
                ob = b - HALO_BLKS
                if ob >= 12:
                    kvout.append((i, kvp, (ob - 12) * 128))
            if smp:
                kvout.append((npb, kvs, 0))
        layer0(kind, blocks, smp, kvout)
        if kind == "kv":
            continue
        out_proj(nb, scrO, KO, ogcz)
        rmsnorm_T(nb, 38)
        layer1(npb, smp, mask_first=(npb > 0 and blocks[0] == 4))
        out_proj(nb, scrP, KP, ogcz)
        dsts = [(yp[(b - HALO_BLKS) * 128:(b - HALO_BLKS + 1) * 128, :] if b >= HALO_BLKS else None) for b in blocks]
        if smp:
            dsts.append(ysd)
        final_norm(nb, dsts)
        if npb and blocks[-1] == NPB - 1:
            for g4 in range(4):
                for c in range(4):
                    cc = g4 * 4 + c
                    P.pe(lambda e, cc=cc, c=c: e.matmul(psS[2][0:CK, c * 128:(c + 1) * 128], lhsT=ust[:, cc, :], rhs=identf[:],
                                                        start=True, stop=True),
                         reads=[("ust", cc), "identf"], writes=[("psS", 2)])
                P.act(lambda e: e.copy(out=cst[0:CK], in_=psS[2][0:CK, 0:512].rearrange("p (s c) -> p s c", c=128)),
                      reads=[("psS", 2)], writes=["cst"])
                P.dma("pool", cpd[:, g4 * 512:(g4 + 1) * 512].rearrange("r (s c) -> r s c", c=128), cst[0:CK], reads=["cst"])

    P.emit()
    return nc, P


_CACHE = {}


def _get_program():
    if "nc" not in _CACHE:
        _CACHE["nc"], _CACHE["P"] = build_program()
    return _CACHE["nc"]


def prepare(x_prompt, x_sample, cache_attn_k, cache_attn_v, cache_conv, norm_g, final_g,
            attn_w_in, attn_w_out, attn_rel_bias, conv_w_in, conv_b_in, conv_w_dw, conv_b_dw,
            conv_ln_g, conv_ln_b, conv_w_out):
    f = lambda a: np.ascontiguousarray(np.asarray(a, dtype=np.float32))
    x_prompt, x_sample = f(x_prompt), f(x_sample)
    ck_all = f(cache_attn_k).reshape(32, 512, D)
    cv_all = f(cache_attn_v).reshape(32, 512, D)
    cc_all = f(cache_conv).reshape(32, CK, D)
    w_ai, w_ao = f(attn_w_in)[0], f(attn_w_out)[0]
    w_ci, w_co = f(conv_w_in)[0], f(conv_w_out)[0]
    table = f(attn_rel_bias)[0]
    prm = np.concatenate([f(conv_w_dw)[0], f(conv_b_dw), f(conv_ln_g), f(conv_ln_b),
                          f(conv_b_in)[0].reshape(3, D), f(norm_g)], axis=0)
    fg = np.ascontiguousarray(np.broadcast_to(f(final_g)[None, :], (128, D)))
    kk = np.arange(128)[:, None]
    qq = np.arange(128)[None, :]
    relT = np.empty((128, 2, NH, 128), np.float32)
    for dl in range(2):
        idx = np.clip(128 * dl + qq - kk, -128, 128) + 128
        relT[:, dl] = np.transpose(table[idx], (0, 2, 1))
    cvec = np.ascontiguousarray(np.broadcast_to(table[256][None, :], (128, NH)))
    ident = np.eye(128, dtype=np.float32)
    xpad = np.concatenate([np.zeros((HALO_BLKS * 128, D), np.float32), x_prompt[0]], axis=0)
    in_maps = []
    for c in range(NCORES):
        s0 = c * OWN
        in_maps.append(dict(
            xp=np.ascontiguousarray(xpad[s0:s0 + NPB * 128]),
            xs=np.ascontiguousarray(x_sample[4 * c:4 * c + 4].reshape(128, D)),
            ck=np.ascontiguousarray(ck_all[4 * c:4 * c + 4]),
            cv=np.ascontiguousarray(cv_all[4 * c:4 * c + 4]),
            ccv=np.ascontiguousarray(cc_all[4 * c:4 * c + 4].reshape(4 * CK, D)),
            w_ai=w_ai, w_ao=w_ao, w_ci=w_ci, w_co=w_co, prm=prm, fg=fg, relT=relT, cvec=cvec, ident=ident,
            hmask=np.full((128, 1), 0.0 if c == 0 else 1.0, np.float32),
        ))
    return in_maps


def assemble(R):
    g = lambda c, k: np.asarray(R[c][k], dtype=np.float32)
    y_prompt = np.concatenate([g(c, "yp") for c in range(NCORES)], axis=0)[None]
    y_sample = np.concatenate([g(c, "ys").reshape(4, 32, D) for c in range(NCORES)], axis=0)
    kvp = g(NCORES - 1, "kvp")
    k_p = kvp[0].reshape(1, 1, 512, NH, HD)
    v_p = kvp[1].reshape(1, 1, 512, NH, HD)
    c_p = g(NCORES - 1, "cp").reshape(1, 1, CK, D)
    ks, vs, cs = [], [], []
    for c in range(NCORES):
        kvn = g(c, "kvs")
        ks.append(np.concatenate([g(c, "kso"), kvn[0].reshape(4, 32, D)], axis=1))
        vs.append(np.concatenate([g(c, "vso"), kvn[1].reshape(4, 32, D)], axis=1))
        cs.append(g(c, "cs"))
    k_s = np.concatenate(ks, axis=0).reshape(1, 32, 512, NH, HD)
    v_s = np.concatenate(vs, axis=0).reshape(1, 32, 512, NH, HD)
    c_s = np.concatenate(cs, axis=0).reshape(1, 32, CK, D)
    return (y_prompt, y_sample, k_p, v_p, c_p, k_s, v_s, c_s)


def kernel(**inputs):
    in_maps = prepare(**inputs)
    nc = _get_program()
    res = run_bass_kernel_spmd(nc, in_maps, core_ids=list(range(NCORES)))
    return assemble(res.results)
```
